# Optimizing a Trainium2 kernel written in Bass

```python
import math
import jax, jax.numpy as jnp
from jax import lax
import numpy as np

D_MODEL = 2048
BATCH = 2
SEQ = 4096
DEPTH = 2
DEC_BATCH = 16
DEC_SEQ = 32
PAST_LEN = 2048

CHUNK = 64
Q_BLOCK = 128
MLA_HEADS = 16
Q_LORA = 512
KV_LORA = 512
NOPE_DIM = 128
ROPE_DIM = 64
V_DIM = 128
ROPE_THETA = 10000.0
MLA_SCALE = (NOPE_DIM + ROPE_DIM) ** -0.5
SB_HEADS = 8
SB_DIM = 128
SB_SCALE = SB_DIM ** -0.5
FF_DIM = -(-8 * D_MODEL // (3 * 256)) * 256
PLE_DIM = 256
ALPHA = (2 * DEPTH) ** 0.25
BETA = (8 * DEPTH) ** -0.25
LN_EPS = 1e-5
RMS_EPS = 1e-6
SB_WIDTH = SB_HEADS * SB_DIM
IN_SIZES = (Q_LORA, KV_LORA, ROPE_DIM, SB_WIDTH, SB_WIDTH, SB_WIDTH, 2 * D_MODEL)
IN_COLS = Q_LORA + KV_LORA + ROPE_DIM + 3 * SB_WIDTH + 2 * D_MODEL
IN_SPLITS = [Q_LORA, Q_LORA + KV_LORA, Q_LORA + KV_LORA + ROPE_DIM,
             Q_LORA + KV_LORA + ROPE_DIM + SB_WIDTH,
             Q_LORA + KV_LORA + ROPE_DIM + 2 * SB_WIDTH,
             Q_LORA + KV_LORA + ROPE_DIM + 3 * SB_WIDTH]

kernel_name = "hybrid_mla_stickbreaking_streaming_step"


def layer_norm(x, g, b):
    xf = x.astype(jnp.float32)
    mu = jnp.mean(xf, -1, keepdims=True)
    var = jnp.mean(jnp.square(xf - mu), -1, keepdims=True)
    return ((xf - mu) * lax.rsqrt(var + LN_EPS) * g + b).astype(x.dtype)


def rms_norm(x, g):
    xf = x.astype(jnp.float32)
    return (xf * lax.rsqrt(jnp.mean(jnp.square(xf), -1, keepdims=True) + RMS_EPS) * g).astype(x.dtype)


def rope(x, pos):
    half = ROPE_DIM // 2
    inv_freq = 1.0 / (ROPE_THETA ** (jnp.arange(half, dtype=jnp.float32) * (2.0 / ROPE_DIM)))
    ang = pos.astype(jnp.float32)[:, None] * inv_freq[None, :]
    ang = ang.reshape(ang.shape[0], *([1] * (x.ndim - 3)), half)
    cos, sin = jnp.cos(ang), jnp.sin(ang)
    x1, x2 = x[..., :half], x[..., half:]
    return jnp.concatenate([x1 * cos - x2 * sin, x2 * cos + x1 * sin], -1).astype(x.dtype)


def over_query_blocks(attend, qs, q_pos):
    n_q = q_pos.shape[0]
    if n_q <= Q_BLOCK:
        return attend(qs, q_pos)
    nb = n_q // Q_BLOCK

    def to_blocks(a):
        return jnp.moveaxis(a.reshape(a.shape[0], nb, Q_BLOCK, *a.shape[2:]), 1, 0)

    out = lax.map(lambda args: attend(args[0], args[1]),
                  (tuple(to_blocks(q) for q in qs), q_pos.reshape(nb, Q_BLOCK)))
    out = jnp.moveaxis(out, 0, 1)
    return out.reshape(out.shape[0], n_q, *out.shape[3:])


def mla_attend(q_lat, q_rope, q_pos, lat, k_rope, k_pos, w_uv):
    s = (jnp.einsum('bqhc,bkc->bhqk', q_lat, lat)
         + jnp.einsum('bqhr,bkr->bhqk', q_rope, k_rope)).astype(jnp.float32) * MLA_SCALE
    visible = (k_pos[None, :] // CHUNK) <= (q_pos[:, None] // CHUNK)
    s = jnp.where(visible, s, -jnp.inf)
    p = jax.nn.softmax(s, axis=-1).astype(lat.dtype)
    ctx = jnp.einsum('bhqk,bkc->bqhc', p, lat)
    return jnp.einsum('bqhc,chv->bqhv', ctx, w_uv)


def sb_attend(q, q_pos, k, v, k_pos):
    z = jnp.einsum('bqhd,bkhd->bhqk', q, k).astype(jnp.float32) * SB_SCALE
    before = k_pos[None, :] < q_pos[:, None]
    log_beta = jax.nn.log_sigmoid(z)
    log_rest = jnp.where(before, jax.nn.log_sigmoid(-z), 0.0)
    tail = lax.cumsum(log_rest, axis=log_rest.ndim - 1, reverse=True)
    tail = jnp.concatenate([tail[..., 1:], jnp.zeros_like(tail[..., :1])], -1)
    a = jnp.where(before, jnp.exp(log_beta + tail), 0.0).astype(v.dtype)
    return jnp.einsum('bhqk,bkhd->bqhd', a, v)


def trunk_layer(x, p, past_lat, past_kr, past_k, past_v,
                w_in, b_gate, q_a_norm_g, w_q_b, kv_a_norm_g, w_kv_b,
                w_branch_a, w_branch_b, w_out, ln1_g, ln1_b,
                w_ffn_gu, w_ffn_down, ln2_g, ln2_b,
                w_ple_gate, b_ple_gate, w_ple_proj, ln3_g, ln3_b):
    b, s, _ = x.shape
    n_past = past_lat.shape[1]
    q_pos = n_past + jnp.arange(s, dtype=jnp.int32)
    k_pos = jnp.arange(n_past + s, dtype=jnp.int32)

    h = x @ w_in
    q_a, kv_a, k_r, sb_q, sb_k, sb_v, gates = jnp.split(h, IN_SPLITS, axis=-1)

    q = (rms_norm(q_a, q_a_norm_g) @ w_q_b).reshape(b, s, MLA_HEADS, NOPE_DIM + ROPE_DIM)
    q_rope = rope(q[..., NOPE_DIM:], q_pos)
    w_kv = w_kv_b.reshape(KV_LORA, MLA_HEADS, NOPE_DIM + V_DIM)
    w_uk, w_uv = w_kv[..., :NOPE_DIM], w_kv[..., NOPE_DIM:]
    q_lat = jnp.einsum('bshn,chn->bshc', q[..., :NOPE_DIM], w_uk)
    new_lat = rms_norm(kv_a, kv_a_norm_g)
    new_kr = rope(k_r, q_pos)
    lat_all = jnp.concatenate([past_lat, new_lat], 1)
    kr_all = jnp.concatenate([past_kr, new_kr], 1)
    o_a = over_query_blocks(
        lambda qs, qp: mla_attend(qs[0], qs[1], qp, lat_all, kr_all, k_pos, w_uv),
        (q_lat, q_rope), q_pos)

    sbq = sb_q.reshape(b, s, SB_HEADS, SB_DIM)
    new_k = sb_k.reshape(b, s, SB_HEADS, SB_DIM)
    new_v = sb_v.reshape(b, s, SB_HEADS, SB_DIM)
    k_all = jnp.concatenate([past_k, new_k], 1)
    v_all = jnp.concatenate([past_v, new_v], 1)
    o_b = over_query_blocks(
        lambda qs, qp: sb_attend(qs[0], qp, k_all, v_all, k_pos), (sbq,), q_pos)

    g_a, g_b = jnp.split(jax.nn.sigmoid(gates + b_gate), 2, axis=-1)
    merged = (g_a * (o_a.reshape(b, s, -1) @ w_branch_a)
              + g_b * (o_b.reshape(b, s, -1) @ w_branch_b))
    x = layer_norm(ALPHA * x + merged @ w_out, ln1_g, ln1_b)

    f_g, f_u = jnp.split(x @ w_ffn_gu, 2, axis=-1)
    x = layer_norm(ALPHA * x + (jax.nn.silu(f_g) * f_u) @ w_ffn_down, ln2_g, ln2_b)

    ple = jax.nn.sigmoid(x @ w_ple_gate + b_ple_gate) * (p @ w_ple_proj)
    x = layer_norm(ALPHA * x + ple, ln3_g, ln3_b)
    return x, (new_lat, new_kr, new_k, new_v)


def setup_inputs(seed: int = 0) -> dict:
    key = jax.random.key(seed)
    ks = iter(jax.random.split(key, 40))

    def nrm(shape, scale=1.0):
        return jax.random.normal(next(ks), shape, jnp.float32) * scale

    def gain(shape):
        return 1.0 + nrm(shape, 0.02)

    L = DEPTH
    return {
        "x_prompt": nrm((BATCH, SEQ, D_MODEL)),
        "x_sample": nrm((DEC_BATCH, DEC_SEQ, D_MODEL)),
        "cache_mla_latent": nrm((L, DEC_BATCH, PAST_LEN, KV_LORA)),
        "cache_mla_krope": nrm((L, DEC_BATCH, PAST_LEN, ROPE_DIM)),
        "cache_sb_k": nrm((L, DEC_BATCH, PAST_LEN, SB_HEADS, SB_DIM)),
        "cache_sb_v": nrm((L, DEC_BATCH, PAST_LEN, SB_HEADS, SB_DIM)),
        "p_prompt": nrm((L, BATCH, SEQ, PLE_DIM)),
        "p_sample": nrm((L, DEC_BATCH, DEC_SEQ, PLE_DIM)),
        "ln_in_g": gain((D_MODEL,)),
        "ln_in_b": nrm((D_MODEL,), 0.02),
        "w_in": nrm((L, D_MODEL, IN_COLS), D_MODEL ** -0.5),
        "b_gate": nrm((L, 2 * D_MODEL), 0.02),
        "q_a_norm_g": gain((L, Q_LORA)),
        "w_q_b": nrm((L, Q_LORA, MLA_HEADS * (NOPE_DIM + ROPE_DIM)), Q_LORA ** -0.5),
        "kv_a_norm_g": gain((L, KV_LORA)),
        "w_kv_b": nrm((L, KV_LORA, MLA_HEADS * (NOPE_DIM + V_DIM)), KV_LORA ** -0.5),
        "w_branch_a": nrm((L, MLA_HEADS * V_DIM, D_MODEL), (MLA_HEADS * V_DIM) ** -0.5),
        "w_branch_b": nrm((L, SB_WIDTH, D_MODEL), SB_WIDTH ** -0.5),
        "w_out": nrm((L, D_MODEL, D_MODEL), BETA * D_MODEL ** -0.5),
        "ln1_g": gain((L, D_MODEL)),
        "ln1_b": nrm((L, D_MODEL), 0.02),
        "w_ffn_gu": nrm((L, D_MODEL, 2 * FF_DIM), D_MODEL ** -0.5),
        "w_ffn_down": nrm((L, FF_DIM, D_MODEL), BETA * FF_DIM ** -0.5),
        "ln2_g": gain((L, D_MODEL)),
        "ln2_b": nrm((L, D_MODEL), 0.02),
        "w_ple_gate": nrm((L, D_MODEL, D_MODEL), D_MODEL ** -0.5),
        "b_ple_gate": nrm((L, D_MODEL), 0.02),
        "w_ple_proj": nrm((L, PLE_DIM, D_MODEL), BETA * PLE_DIM ** -0.5),
        "ln3_g": gain((L, D_MODEL)),
        "ln3_b": nrm((L, D_MODEL), 0.02),
    }


def reference(x_prompt, x_sample, cache_mla_latent, cache_mla_krope, cache_sb_k, cache_sb_v,
              p_prompt, p_sample, ln_in_g, ln_in_b, w_in, b_gate, q_a_norm_g, w_q_b,
              kv_a_norm_g, w_kv_b, w_branch_a, w_branch_b, w_out, ln1_g, ln1_b,
              w_ffn_gu, w_ffn_down, ln2_g, ln2_b, w_ple_gate, b_ple_gate, w_ple_proj,
              ln3_g, ln3_b):
    def run(x, p, c_lat, c_kr, c_k, c_v):
        x = layer_norm(x, ln_in_g, ln_in_b)
        rows = []
        for i in range(DEPTH):
            x, r = trunk_layer(x, p[i], c_lat[i], c_kr[i], c_k[i], c_v[i],
                               w_in[i], b_gate[i], q_a_norm_g[i], w_q_b[i],
                               kv_a_norm_g[i], w_kv_b[i], w_branch_a[i], w_branch_b[i],
                               w_out[i], ln1_g[i], ln1_b[i], w_ffn_gu[i], w_ffn_down[i],
                               ln2_g[i], ln2_b[i], w_ple_gate[i], b_ple_gate[i],
                               w_ple_proj[i], ln3_g[i], ln3_b[i])
            rows.append(r)
        lat, kr, k, v = (jnp.stack(t) for t in zip(*rows))
        return x, lat, kr, k, v

    bp = x_prompt.shape[0]
    dt = x_prompt.dtype
    y_prompt, lat_p, kr_p, k_p, v_p = run(
        x_prompt, p_prompt,
        jnp.zeros((DEPTH, bp, 0, KV_LORA), dt), jnp.zeros((DEPTH, bp, 0, ROPE_DIM), dt),
        jnp.zeros((DEPTH, bp, 0, SB_HEADS, SB_DIM), dt), jnp.zeros((DEPTH, bp, 0, SB_HEADS, SB_DIM), dt))
    y_sample, lat_s, kr_s, k_s, v_s = run(
        x_sample, p_sample, cache_mla_latent, cache_mla_krope, cache_sb_k, cache_sb_v)
    return (y_prompt, y_sample, lat_p, kr_p, k_p, v_p, lat_s, kr_s, k_s, v_s)
```

```python
import numpy as np
import ml_dtypes
import concourse.bass as bass
import concourse.mybir as mybir
from concourse.bass_utils import run_bass_kernel_spmd

F32 = mybir.dt.float32
BF16 = mybir.dt.bfloat16
AF = mybir.ActivationFunctionType
ALU = mybir.AluOpType
AX = mybir.AxisListType

D = 2048; SEQ = 4096; DEPTH = 2; PAST = 2048; DSEQ = 32
QL = 512; KVL = 512; ROPE = 64; NOPE = 128; VD = 128; MH = 16
SBH = 8; SBD = 128; SBW = 1024; FF = 5632; PLE = 256
INC = 8256
MLA_SCALE = float((NOPE + ROPE) ** -0.5)
SB_SCALE = float(SBD ** -0.5)
ALPHA = float((2 * DEPTH) ** 0.25)
LN_EPS = 1e-5; RMS_EPS = 1e-6
NT = 4
TOK = NT * 128
NPASS = SEQ // TOK
SKEYS = 2176
NWB = 4
WBE = 8448


COMPUTE = ("pe", "act", "dve")


class Tracker:
    def __init__(self, nc, plan=None):
        self.nc = nc
        self.plan = plan
        self.dry = plan is None
        self.n = 0
        self.last_w = {}
        self.readers = {}
        self.deps = []
        self.unit = []
        self.queue = []
        self.last_on = {}
        self.dma_since_bar = []

    def setup(self, es):
        nc, plan = self.nc, self.plan
        if not self.dry:
            sem = lambda name: es.enter_context(nc.semaphore(name))
            self.sems = {e: sem(f"s_{e}") for e in COMPUTE}
            self.bar = sem("s_bar")
            self.dsems = {q: [sem(f"d_{q}{i}") for i in range(plan["R"])] for q in ("sp", "pool")}
            self.eng = {"pe": nc.tensor, "act": nc.scalar, "dve": nc.vector, "sp": nc.sync, "pool": nc.gpsimd}

    def _record(self, unit, reads, writes, queue):
        i = self.n
        d = set()
        for r in reads:
            w = self.last_w.get(r)
            if w is not None:
                d.add(w)
        for k in writes:
            w = self.last_w.get(k)
            if w is not None:
                d.add(w)
            rd = self.readers.get(k)
            if rd:
                d.update(rd.values())
        for r in reads:
            self.readers.setdefault(r, {})[unit if unit != "dma" else ("dma", i)] = i
        for k in writes:
            self.last_w[k] = i
            self.readers[k] = {}
        d.discard(i)
        self.deps.append(d)
        self.unit.append(unit)
        self.queue.append(queue)
        if unit == "dma":
            self.dma_since_bar.append(i)
        else:
            self.last_on[unit] = i

    def op(self, unit, fn, reads=(), writes=()):
        if self.dry:
            self._record(unit, reads, writes, None)
        else:
            self._emit(fn)
        self.n += 1

    def dma(self, queue, out, in_, reads=(), writes=()):
        if self.dry:
            self._record("dma", reads, writes, queue)
        else:
            self._emit(lambda: self.eng[queue].dma_start(out=out, in_=in_))
        self.n += 1

    def barrier(self):
        if self.dry:
            i = self.n
            d = set(self.last_on.values()) | set(self.dma_since_bar)
            self.dma_since_bar = []
            self.deps.append(d)
            self.unit.append("bar")
            self.queue.append("sp")
        else:
            self._emit(None)
        self.n += 1

    def make_plan(self, R=4):
        n = self.n
        unit, queue, deps = self.unit, self.queue, self.deps
        needs_inc = [False] * n
        for i in range(n):
            for j in deps[i]:
                if unit[j] in COMPUTE:
                    if unit[j] == unit[i] and unit[i] == "pe":
                        continue
                    needs_inc[j] = True
        cnt = {e: 0 for e in COMPUTE}
        incidx = [0] * n
        dcount = {"sp": 0, "pool": 0}
        dslot = [None] * n
        nbar = 0
        for i in range(n):
            u = unit[i]
            if u in COMPUTE:
                if needs_inc[i]:
                    cnt[u] += 1
                incidx[i] = cnt[u]
            elif u == "dma":
                q = queue[i]
                k = dcount[q]
                dcount[q] += 1
                dslot[i] = (q, k % R, 16 * (k // R + 1))
        waited = {}
        waits = [None] * n
        for i in range(n):
            u = unit[i]
            stream = u if u in COMPUTE else queue[i]
            need = {}
            for j in deps[i]:
                uj = unit[j]
                if uj in COMPUTE:
                    if uj == u and u == "pe":
                        continue
                    key = ("c", uj)
                    need[key] = max(need.get(key, 0), incidx[j])
                elif uj == "dma":
                    q, s, v = dslot[j]
                    key = ("d", q, s)
                    need[key] = max(need.get(key, 0), v)
            if u == "dma":
                q, s, v = dslot[i]
                if v > 16:
                    key = ("d", q, s)
                    need[key] = max(need.get(key, 0), v - 16)
            w = []
            wd = waited.setdefault(stream, {})
            for key, v in need.items():
                if wd.get(key, 0) < v:
                    wd[key] = v
                    w.append((key, v))
            if u == "bar":
                nbar += 1
                waits[i] = (w, nbar)
                for st in COMPUTE + ("pool",):
                    ws = waited.setdefault(st, {})
                    for key, v in wd.items():
                        if ws.get(key, 0) < v:
                            ws[key] = v
            else:
                waits[i] = w
        return {"R": R, "needs_inc": needs_inc, "dslot": dslot, "waits": waits, "unit": unit,
                "queue": queue, "n": n, "dcount": dcount}

    def _sem(self, key):
        if key[0] == "c":
            return self.sems[key[1]]
        return self.dsems[key[1]][key[2]]

    def _emit(self, fn):
        i = self.n
        p = self.plan
        u = p["unit"][i]
        if u == "bar":
            w, k = p["waits"][i]
            for key, v in w:
                self.nc.sync.wait_ge(self._sem(key), v)
            self.nc.sync.sem_inc(self.bar, 1)
            for e in COMPUTE + ("pool",):
                self.eng[e].wait_ge(self.bar, k)
            return
        stream = u if u in COMPUTE else p["queue"][i]
        eng = self.eng[stream]
        ws = p["waits"][i]
        for key, v in ws[:-1]:
            eng.wait_ge(self._sem(key), v)
        ins = fn()
        if ws:
            key, v = ws[-1]
            ins._wait_ge(self._sem(key), v)
        if u in COMPUTE:
            if p["needs_inc"][i]:
                ins.then_inc(self.sems[u], 1)
        else:
            q, s, v = p["dslot"][i]
            ins.then_inc(self.dsems[q][s], 16)

    def finish(self):
        if self.dry:
            return
        p = self.plan
        R = p["R"]
        for q in ("sp", "pool"):
            k = p["dcount"][q]
            for s in range(R):
                cntq = (k - s + R - 1) // R if k > s else 0
                if cntq > 0:
                    self.nc.sync.wait_ge(self.dsems[q][s], 16 * cntq)


class Builder:
    def __init__(self, nc, T):
        self.nc = nc
        self.T = T
        self.ps_i = 0
        self.wb_i = 0
        self.ps_held = set()
        self.wb_held = set()

    def dram_in(self, name, shape, dt=F32):
        return self.nc.dram_tensor(name, list(shape), dt, kind="ExternalInput").ap()

    def dram_out(self, name, shape):
        return self.nc.dram_tensor(name, list(shape), F32, kind="ExternalOutput").ap()

    def scratch(self, name, shape, dt=BF16):
        return self.nc.dram_tensor(name, list(shape), dt).ap()

    def sb(self, name, shape, dt):
        return self.es.enter_context(self.nc.sbuf_tensor(name, list(shape), dt))

    def psum(self, hold=False):
        while True:
            i = self.ps_i % 8
            self.ps_i += 1
            if i not in self.ps_held:
                break
        if hold:
            self.ps_held.add(i)
        return self.ps[i], ("ps", i)

    def wbuf(self, hold=False):
        while True:
            i = self.wb_i % NWB
            self.wb_i += 1
            if i not in self.wb_held:
                break
        if hold:
            self.wb_held.add(i)
        return self.wb[i], ("wb", i)

    def release(self, key):
        (self.ps_held if key[0] == "ps" else self.wb_held).discard(key[1])

    def declare(self):
        L = DEPTH
        self.xp = self.dram_in("xp", [SEQ, D])
        self.xs = self.dram_in("xs", [64, D])
        self.c_lat = self.dram_in("c_lat", [L, 2, PAST, KVL])
        self.c_kr = self.dram_in("c_kr", [L, 2, PAST, ROPE])
        self.c_sbk = self.dram_in("c_sbk", [L, 2, PAST, SBW])
        self.c_sbv = self.dram_in("c_sbv", [L, 2, PAST, SBW])
        self.pp = self.dram_in("pp", [L, SEQ, PLE])
        self.psm = self.dram_in("psm", [L, 64, PLE])
        self.w = {}
        for name, shape in [("ln_in_g", [D]), ("ln_in_b", [D]), ("w_in", [L, D, INC]), ("b_gate", [L, 2 * D]),
                            ("q_a_norm_g", [L, QL]), ("w_q_b", [L, QL, MH * 192]), ("kv_a_norm_g", [L, KVL]),
                            ("w_kv_b", [L, KVL, MH * 256]), ("w_branch_a", [L, D, D]), ("w_branch_b", [L, SBW, D]),
                            ("w_out", [L, D, D]), ("ln1_g", [L, D]), ("ln1_b", [L, D]), ("w_ffn_gu", [L, D, 2 * FF]),
                            ("w_ffn_down", [L, FF, D]), ("ln2_g", [L, D]), ("ln2_b", [L, D]),
                            ("w_ple_gate", [L, D, D]), ("b_ple_gate", [L, D]), ("w_ple_proj", [L, PLE, D]),
                            ("ln3_g", [L, D]), ("ln3_b", [L, D])]:
            self.w[name] = self.dram_in(name, shape)
        self.cosT = self.dram_in("cosT", [SEQ + 64, 32])
        self.sinT = self.dram_in("sinT", [SEQ + 64, 32])
        self.cos2 = self.dram_in("cos2", [64, SEQ + 64])
        self.sin2 = self.dram_in("sin2", [64, SEQ + 64])
        self.cmask = self.dram_in("cmask", [128, 3, 128], BF16)
        self.ident_d = self.dram_in("ident", [128, 128], BF16)
        self.y_p = self.dram_out("y_p", [SEQ, D])
        self.y_s = self.dram_out("y_s", [64, D])
        self.o_lat = self.dram_out("o_lat", [L, SEQ, KVL])
        self.o_kr = self.dram_out("o_kr", [L, SEQ, ROPE])
        self.o_sbk = self.dram_out("o_sbk", [L, SEQ, SBW])
        self.o_sbv = self.dram_out("o_sbv", [L, SEQ, SBW])
        self.os_lat = self.dram_out("os_lat", [L, 64, KVL])
        self.os_kr = self.dram_out("os_kr", [L, 64, ROPE])
        self.os_sbk = self.dram_out("os_sbk", [L, 64, SBW])
        self.os_sbv = self.dram_out("os_sbv", [L, 64, SBW])
        self.kv = {}
        for l in range(L):
            for s in range(3):
                K = SEQ if s == 0 else SKEYS
                self.kv[(l, s)] = dict(
                    knT=self.scratch(f"knT{l}{s}", [MH, 128, K]), krT=self.scratch(f"krT{l}{s}", [64, K]),
                    v=self.scratch(f"v{l}{s}", [K, MH * VD]), sbkT=self.scratch(f"sbkT{l}{s}", [SBH, 128, K]),
                    sbv=self.scratch(f"sbv{l}{s}", [K, SBW]))

    def alloc(self):
        nc = self.nc
        self.ps = [self.es.enter_context(nc.psum_tensor(f"ps{i}", [128, 512], F32)) for i in range(8)]
        self.wb = [self.sb(f"wb{i}", [128, WBE], BF16) for i in range(NWB)]
        self.x = self.sb("x", [128, NT, D], F32)
        self.xT = self.sb("xT", [128, 16, TOK], BF16)
        self.a1 = self.sb("a1", [128, 24 * TOK], BF16)
        self.a3 = self.sb("a3", [128, 16 * TOK], BF16)
        self.tmp = self.sb("tmp", [128, 6144], F32)
        self.ident = self.sb("identb", [128, 128], BF16)
        self.masks = self.sb("masks", [128, 3, 128], BF16)
        self.ones = self.sb("ones", [128, 128], BF16)
        self.small = self.sb("small", [128, 64], F32)
        self.gq = self.sb("gq", [128, 2, 512], F32)
        self.ropeT = self.sb("ropeT", [128, NT, 2, 32], F32)
        self.rope2 = self.sb("rope2", [64, 2, TOK], F32)
        self.bcol = self.sb("bcol", [128, 2, 32], F32)
        self.qh = self.sb("qh", [128, 2, 3, 128 * NT], BF16)
        self.wrot = self.sb("wrot", [128, 2, 4, 64], BF16)
        self.pt = self.sb("pt", [128, 3, 512], BF16)
        self.vaug_ones_done = set()

    def consts(self):
        T, nc = self.T, self.nc
        T.dma("sp", self.ident[:], self.ident_d[:, :], writes=[("ident",)])
        T.dma("sp", self.masks[:], self.cmask[:, :, :], writes=[("masks",)])
        T.op("dve", lambda: nc.vector.memset(self.ones[:], 1.0), writes=[("ones",)])

    def bcast_load(self, dst, src_row, n, key):
        self.T.dma("sp", dst, src_row.partition_broadcast(128), writes=[key])

    def transpose_to(self, src, rows, cols, dst, rkeys, wkeys, evac="act"):
        T, nc = self.T, self.nc
        ps, pk = self.psum()
        pv = ps[:].bitcast(BF16)
        T.op("pe", lambda: nc.tensor.transpose(pv[0:cols, 0:rows], src, self.ident[0:rows, 0:rows]),
             reads=list(rkeys) + [("ident",)], writes=[pk])
        if evac == "act":
            T.op("act", lambda: nc.scalar.copy(out=dst, in_=pv[0:cols, 0:rows]), reads=[pk], writes=wkeys)
        else:
            T.op("dve", lambda: nc.vector.tensor_copy(out=dst, in_=pv[0:cols, 0:rows]), reads=[pk], writes=wkeys)

    def transposes4(self, srcs, rows, dst4, rkeys, wkeys, evac="act"):
        T, nc = self.T, self.nc
        n = len(srcs)
        ps, pk = self.psum()
        pv = ps[:].bitcast(BF16).rearrange("p (n r) -> p n r", n=8)
        for i, s in enumerate(srcs):
            T.op("pe", lambda i=i, s=s: nc.tensor.transpose(pv[:, i, 0:rows], s, self.ident[0:rows, 0:rows]),
                 reads=list(rkeys) + [("ident",)], writes=[pk])
        if evac == "act":
            T.op("act", lambda: nc.scalar.copy(out=dst4, in_=pv[:, 0:n, 0:rows]), reads=[pk], writes=wkeys)
        else:
            T.op("dve", lambda: nc.vector.tensor_copy(out=dst4, in_=pv[:, 0:n, 0:rows]), reads=[pk], writes=wkeys)

    def layer_norm(self, t, rows, gname, bname, l, make_xT=True, out_dram=None):
        T, nc = self.T, self.nc
        xt = self.x[0:rows, t, :]
        st = self.small[0:rows, 0:24].rearrange("p (n s) -> p n s", s=6)
        mv = self.small[0:rows, 24:26]
        rs = self.small[0:rows, 26:27]
        nb = self.small[0:rows, 27:28]
        kx = ("x", t)
        for c in range(4):
            T.op("dve", lambda c=c: nc.vector.bn_stats(out=st[:, c, :], in_=xt[:, c * 512:(c + 1) * 512]),
                 reads=[kx], writes=[("lnst",)])
        T.op("dve", lambda: nc.vector.bn_aggr(out=mv, in_=st), reads=[("lnst",)], writes=[("lnmv",)])
        T.op("dve", lambda: nc.vector.tensor_scalar(out=rs, in0=mv[:, 1:2], scalar1=LN_EPS, scalar2=None,
                                                    op0=ALU.add), reads=[("lnmv",)], writes=[("lnrs",)])
        T.op("act", lambda: nc.scalar.activation(out=rs, in_=rs, func=AF.Sqrt), reads=[("lnrs",)], writes=[("lnrs",)])
        T.op("dve", lambda: nc.vector.reciprocal(out=rs, in_=rs), reads=[("lnrs",)], writes=[("lnrs",)])
        T.op("dve", lambda: nc.vector.scalar_tensor_tensor(out=nb, in0=mv[:, 0:1], scalar=-1.0, in1=rs,
                                                           op0=ALU.mult, op1=ALU.mult),
             reads=[("lnmv",), ("lnrs",)], writes=[("lnnb",)])
        T.op("act", lambda: nc.scalar.activation(out=xt, in_=xt, func=AF.Identity, bias=nb, scale=rs),
             reads=[kx, ("lnrs",), ("lnnb",)], writes=[kx])
        gb = self.lnw
        T.op("dve", lambda: nc.vector.tensor_tensor(out=xt, in0=xt, in1=gb[0:rows, 0, :], op=ALU.mult),
             reads=[kx, ("lnw",), self.lnw_key], writes=[kx])
        T.op("dve", lambda: nc.vector.tensor_tensor(out=xt, in0=xt, in1=gb[0:rows, 1, :], op=ALU.add),
             reads=[kx, ("lnw",), self.lnw_key], writes=[kx])
        if out_dram is not None:
            T.dma("sp", out_dram, xt, reads=[kx])
        if make_xT:
            self.make_xT(t, rows)

    def load_lnw(self, gname, bname, l):
        buf, bk = self.wbuf()
        v = buf[:, 0:8192].bitcast(F32).rearrange("p (a d) -> p a d", a=2)
        g = self.w[gname] if l is None else self.w[gname][l]
        b = self.w[bname] if l is None else self.w[bname][l]
        self.T.dma("sp", v[:, 0, :], g.partition_broadcast(128), writes=[bk, ("lnw",)])
        self.T.dma("sp", v[:, 1, :], b.partition_broadcast(128), writes=[bk, ("lnw",)])
        self.lnw = v
        self.lnw_key = bk

    def make_xT(self, t, rows):
        T, nc = self.T, self.nc
        xb = self.tmp[:, 0:1024].bitcast(BF16)
        T.op("act", lambda: nc.scalar.copy(out=xb[0:rows, :], in_=self.x[0:rows, t, :]),
             reads=[("x", t)], writes=[("xb",), ("R0",), ("R1",)])
        for g in range(2):
            srcs = [xb[0:rows, (g * 8 + i) * 128:(g * 8 + i + 1) * 128] for i in range(8)]
            self.transposes4(srcs, rows, self.xT[:, g * 8:(g + 1) * 8, t * 128:t * 128 + rows],
                             [("xb",)], [("xT", t)], evac="dve" if g else "act")

    def load_w(self, src, kchunks, cols, queue="pool"):
        buf, bk = self.wbuf()
        v = buf[:, 0:kchunks * cols].rearrange("p (k c) -> p k c", k=kchunks)
        self.T.dma(queue, v, src.rearrange("(k p) c -> p k c", p=128), writes=[bk])
        return v, bk

    def stage_A(self, l, rows_list, src_list, pos0, sample):
        T, nc = self.T, self.nc
        ntl = len(rows_list)
        ntok = TOK if not sample else 64
        w_in = self.w["w_in"][l]
        self.q_aT = self.a3[:, 0:4 * TOK].rearrange("p (c t) -> p c t", c=4)
        self.sbqT = self.a3[:, 4 * TOK:12 * TOK].rearrange("p (h t) -> p h t", h=8)
        stg = self.a1
        lat_bf = stg[:, 0:NT * 512].rearrange("p (t c) -> p t c", t=NT)
        kr_bf = stg[:, NT * 512:NT * 576].rearrange("p (t c) -> p t c", t=NT)
        sbk_bf = stg[:, NT * 576:NT * 1600].rearrange("p (t c) -> p t c", t=NT)
        sbv_bf = stg[:, NT * 1600:NT * 2624].rearrange("p (t c) -> p t c", t=NT)
        f32t = self.tmp[:, 1024:1536]
        f32u = self.tmp[:, 1536:2048]
        sq = self.tmp[:, 2048:2560]
        bft = self.tmp[:, 2560:2816].bitcast(BF16)
        if sample:
            o_lat, o_kr, o_sbk, o_sbv = self.os_lat[l], self.os_kr[l], self.os_sbk[l], self.os_sbv[l]
            r0 = 0
        else:
            o_lat, o_kr, o_sbk, o_sbv = self.o_lat[l], self.o_kr[l], self.o_sbk[l], self.o_sbv[l]
            r0 = pos0
        self.bcast_load(self.gq[:, 0, :], self.w["q_a_norm_g"][l], 512, ("gq",))
        self.bcast_load(self.gq[:, 1, :], self.w["kv_a_norm_g"][l], 512, ("gq",))
        cgs = [(0, 512, "qa"), (512, 512, "kva"), (1024, 64, "kr"), (1088, 512, "sbq0"), (1600, 512, "sbq1"),
               (2112, 512, "sbk0"), (2624, 512, "sbk1"), (3136, 512, "sbv0"), (3648, 512, "sbv1")]
        fl = [0]

        def stage_buf():
            fl[0] ^= 1
            return (f32t, ("f32t",)) if fl[0] else (f32u, ("f32u",))

        import os
        cgs = cgs[:int(os.environ.get("KCG", "9"))]
        for (c0, cw, kind) in cgs:
            wv, wk = self.load_w(w_in[:, c0:c0 + cw], 16, cw)
            for t in range(ntl):
                rows = rows_list[t]
                ps, pk = self.psum()
                for k in range(16):
                    T.op("pe", lambda k=k, t=t, rows=rows, ps=ps, wv=wv, cw=cw: nc.tensor.matmul(
                        ps[0:rows, 0:cw], self.xT[:, k, t * 128:t * 128 + rows], wv[:, k, :],
                        start=(k == 0), stop=(k == 15)), reads=[("xT", t), wk], writes=[pk])
                orow = slice(r0 + t * 128, r0 + t * 128 + rows)
                if kind in ("qa", "kva"):
                    gi = 0 if kind == "qa" else 1
                    ss = self.small[0:rows, 32 + gi:33 + gi]
                    T.op("dve", lambda ss=ss: nc.vector.memset(ss, 0.0), writes=[("ss", gi)])
                    T.op("act", lambda ps=ps, rows=rows, ss=ss: nc.scalar.activation(
                        out=sq[0:rows, :], in_=ps[0:rows, :], func=AF.Square, accum_out=ss),
                        reads=[pk, ("ss", gi)], writes=[("sq",), ("ss", gi)])
                    T.op("dve", lambda ss=ss: nc.vector.tensor_scalar(out=ss, in0=ss, scalar1=1.0 / 512, scalar2=RMS_EPS,
                                                                      op0=ALU.mult, op1=ALU.add),
                         reads=[("ss", gi)], writes=[("ss", gi)])
                    T.op("act", lambda ss=ss: nc.scalar.activation(out=ss, in_=ss, func=AF.Sqrt),
                         reads=[("ss", gi)], writes=[("ss", gi)])
                    T.op("dve", lambda ss=ss: nc.vector.reciprocal(out=ss, in_=ss), reads=[("ss", gi)], writes=[("ss", gi)])
                    if kind == "qa":
                        T.op("dve", lambda ps=ps, rows=rows, ss=ss: nc.vector.scalar_tensor_tensor(
                            out=bft[0:rows, :], in0=ps[0:rows, :], scalar=ss, in1=self.gq[0:rows, 0, :],
                            op0=ALU.mult, op1=ALU.mult), reads=[pk, ("ss", gi), ("gq",)], writes=[("bft",)])
                        srcs = [bft[0:rows, i * 128:(i + 1) * 128] for i in range(4)]
                        self.transposes4(srcs, rows, self.q_aT[:, :, t * 128:t * 128 + rows], [("bft",)], [("q_aT", t)])
                    else:
                        fb, fk = stage_buf()
                        T.op("dve", lambda ps=ps, rows=rows, ss=ss, fb=fb: nc.vector.scalar_tensor_tensor(
                            out=fb[0:rows, :], in0=ps[0:rows, :], scalar=ss, in1=self.gq[0:rows, 1, :],
                            op0=ALU.mult, op1=ALU.mult), reads=[pk, ("ss", gi), ("gq",)], writes=[fk])
                        T.dma("sp", o_lat[orow, :], fb[0:rows, :], reads=[fk])
                        T.op("act", lambda rows=rows, t=t, fb=fb: nc.scalar.copy(out=lat_bf[0:rows, t, :], in_=fb[0:rows, :]),
                             reads=[fk], writes=[("kstg", t)])
                elif kind == "kr":
                    fb, fk = stage_buf()
                    x1 = ps[0:rows, 0:32]; x2 = ps[0:rows, 32:64]
                    cs = self.ropeT[0:rows, t, 0, :]; sn = self.ropeT[0:rows, t, 1, :]
                    ta = sq[0:rows, 0:32]; tb = sq[0:rows, 32:64]
                    T.op("dve", lambda: nc.vector.tensor_tensor(out=ta, in0=x1, in1=cs, op=ALU.mult),
                         reads=[pk, ("ropeT",)], writes=[("sq",)])
                    T.op("dve", lambda: nc.vector.tensor_tensor(out=tb, in0=x2, in1=sn, op=ALU.mult),
                         reads=[pk, ("ropeT",)], writes=[("sq",)])
                    T.op("dve", lambda: nc.vector.tensor_tensor(out=fb[0:rows, 0:32], in0=ta, in1=tb, op=ALU.subtract),
                         reads=[("sq",)], writes=[fk])
                    T.op("dve", lambda: nc.vector.tensor_tensor(out=ta, in0=x2, in1=cs, op=ALU.mult),
                         reads=[pk, ("ropeT",), fk], writes=[("sq",)])
                    T.op("dve", lambda: nc.vector.tensor_tensor(out=tb, in0=x1, in1=sn, op=ALU.mult),
                         reads=[pk, ("ropeT",)], writes=[("sq",)])
                    T.op("dve", lambda: nc.vector.tensor_tensor(out=fb[0:rows, 32:64], in0=ta, in1=tb, op=ALU.add),
                         reads=[("sq",)], writes=[fk])
                    T.dma("sp", o_kr[orow, :], fb[0:rows, 0:64], reads=[fk])
                    T.op("act", lambda rows=rows, t=t, fb=fb: nc.scalar.copy(out=kr_bf[0:rows, t, :], in_=fb[0:rows, 0:64]),
                         reads=[fk], writes=[("kstg", t)])
                elif kind.startswith("sbq"):
                    hh = int(kind[-1])
                    T.op("act", lambda ps=ps, rows=rows: nc.scalar.copy(out=bft[0:rows, :], in_=ps[0:rows, :]),
                         reads=[pk], writes=[("bft",)])
                    srcs = [bft[0:rows, i * 128:(i + 1) * 128] for i in range(4)]
                    self.transposes4(srcs, rows, self.sbqT[:, hh * 4:hh * 4 + 4, t * 128:t * 128 + rows],
                                     [("bft",)], [("sbqT", t)], evac="dve")
                else:
                    hh = int(kind[-1])
                    isk = kind.startswith("sbk")
                    fb, fk = stage_buf()
                    dst_bf = (sbk_bf if isk else sbv_bf)
                    odr = (o_sbk if isk else o_sbv)
                    T.op("act", lambda ps=ps, rows=rows, fb=fb: nc.scalar.copy(out=fb[0:rows, :], in_=ps[0:rows, :]),
                         reads=[pk], writes=[fk])
                    T.dma("sp", odr[orow, hh * 512:(hh + 1) * 512], fb[0:rows, :], reads=[fk])
                    T.op("dve", lambda fb=fb, rows=rows, t=t, dst_bf=dst_bf, hh=hh: nc.vector.tensor_copy(
                        out=dst_bf[0:rows, t, hh * 512:(hh + 1) * 512], in_=fb[0:rows, :]),
                        reads=[fk], writes=[("kstg", t)])
        import os
        if os.environ.get("KNOKPREP"):
            return
        self.load_wkv(l)
        if not sample:
            for t in range(ntl):
                self.kprep(l, 0, pos0 + t * 128, 128, lat_bf[:, t, :], kr_bf[:, t, :], sbk_bf[:, t, :], sbv_bf[:, t, :],
                           [("kstg", t)])
        else:
            self.kprep(l, 1, PAST, 32, lat_bf[:, 0, :], kr_bf[:, 0, :], sbk_bf[:, 0, :], sbv_bf[:, 0, :], [("kstg", 0)])
            self.kprep(l, 2, PAST, 32, lat_bf[:, 0, :], kr_bf[:, 0, :], sbk_bf[:, 0, :], sbv_bf[:, 0, :], [("kstg", 0)],
                       prow=32)
        self.release(self.wuk_k)
        self.release(self.wuv_k)

    def load_wkv(self, l):
        wkv = self.w["w_kv_b"][l].rearrange("(k p) (h t d) -> p k h t d", p=128, h=MH, t=2)
        b0, k0 = self.wbuf(hold=True)
        b1, k1 = self.wbuf(hold=True)
        self.wuk = b0[:, 0:4 * 2048].rearrange("p (k h d) -> p k h d", k=4, h=MH)
        self.wuv = b1[:, 0:4 * 2048].rearrange("p (k h d) -> p k h d", k=4, h=MH)
        for k in range(4):
            self.T.dma("pool", self.wuk[:, k, :, :], wkv[:, k, :, 0, :], writes=[k0])
            self.T.dma("pool", self.wuv[:, k, :, :], wkv[:, k, :, 1, :], writes=[k1])
        self.wuk_k, self.wuv_k = k0, k1

    def kprep(self, l, s, key0, nk, lat_bf, kr_bf, sbk_bf, sbv_bf, rkeys, prow=0):
        T, nc = self.T, self.nc
        kv = self.kv[(l, s)]
        kk = ("kv", l, s)
        pr = slice(prow, prow + nk)
        wk_ = list(rkeys)
        if prow != 0:
            bnc = self.bounce
            T.dma("sp", bnc[0:nk, 0:512], lat_bf[pr, :], reads=wk_, writes=[("bnc",)])
            T.dma("sp", bnc[0:nk, 512:576], kr_bf[pr, :], reads=wk_, writes=[("bnc",)])
            T.dma("sp", bnc[0:nk, 576:1600], sbk_bf[pr, :], reads=wk_, writes=[("bnc",)])
            T.dma("sp", bnc[0:nk, 1600:2624], sbv_bf[pr, :], reads=wk_, writes=[("bnc",)])
            st2 = self.tmp[:, 1024:2336].bitcast(BF16)
            T.dma("sp", st2[0:nk, :], bnc[0:nk, :], reads=[("bnc",)], writes=[("st2",), ("f32t",), ("f32u",), ("sq",)])
            lat_bf, kr_bf, sbk_bf, sbv_bf = st2[:, 0:512], st2[:, 512:576], st2[:, 576:1600], st2[:, 1600:2624]
            wk_ = [("st2",)]
            pr = slice(0, nk)
        ksl = slice(key0, key0 + nk)
        T.dma("sp", kv["sbv"][ksl, :], sbv_bf[pr, :], reads=wk_, writes=[kk])
        latT = self.tmp[:, 2816:3072].bitcast(BF16).rearrange("p (c k) -> p c k", c=4)
        self.transposes4([lat_bf[pr, c * 128:(c + 1) * 128] for c in range(4)], nk, latT[:, :, 0:nk], wk_, [("latT",)])
        krT = self.tmp[:, 3072:3136].bitcast(BF16)
        self.transpose_to(kr_bf[pr, :], nk, 64, krT[0:64, 0:nk], wk_, [("krTs",)], evac="dve")
        T.dma("sp", kv["krT"][:, ksl], krT[0:64, 0:nk], reads=[("krTs",)], writes=[kk])
        sbkT = self.tmp[:, 3136:3648].bitcast(BF16).rearrange("p (h k) -> p h k", h=8)
        self.transposes4([sbk_bf[pr, h * 128:(h + 1) * 128] for h in range(8)], nk, sbkT[:, :, 0:nk], wk_, [("sbkTs",)],
                         evac="dve")
        T.dma("sp", kv["sbkT"][:, :, ksl].rearrange("h p k -> p h k"), sbkT[:, :, 0:nk], reads=[("sbkTs",)], writes=[kk])
        knTs = self.tmp[:, 3648:4672].bitcast(BF16).rearrange("p (h k) -> p h k", h=16)
        for hg in range(4):
            ps, pk = self.psum()
            pv = ps[:].rearrange("p (h k) -> p h k", h=4)
            for hh in range(4):
                h = hg * 4 + hh
                for c in range(4):
                    T.op("pe", lambda h=h, hh=hh, c=c, pv=pv: nc.tensor.matmul(
                        pv[:, hh, 0:nk], self.wuk[:, c, h, :], latT[:, c, 0:nk], start=(c == 0), stop=(c == 3)),
                        reads=[("latT",), self.wuk_k], writes=[pk])
            T.op("act" if hg % 2 == 0 else "dve",
                 (lambda pv=pv, hg=hg: nc.scalar.copy(out=knTs[:, hg * 4:hg * 4 + 4, 0:nk], in_=pv[:, :, 0:nk])) if hg % 2 == 0
                 else (lambda pv=pv, hg=hg: nc.vector.tensor_copy(out=knTs[:, hg * 4:hg * 4 + 4, 0:nk], in_=pv[:, :, 0:nk])),
                 reads=[pk], writes=[("knTs", hg)])
        T.dma("sp", kv["knT"][:, :, ksl].rearrange("h p k -> p h k"), knTs[:, :, 0:nk],
              reads=[("knTs", g) for g in range(4)], writes=[kk])
        vs = self.tmp[:, 4672:5696].bitcast(BF16)
        for cg in range(4):
            ps, pk = self.psum()
            for c in range(4):
                T.op("pe", lambda c=c, cg=cg, ps=ps: nc.tensor.matmul(
                    ps[0:nk, :], latT[:, c, 0:nk], self.wuv[:, c, cg * 4:cg * 4 + 4, :].rearrange("p h d -> p (h d)"),
                    start=(c == 0), stop=(c == 3)), reads=[("latT",), self.wuv_k], writes=[pk])
            T.op("act" if cg % 2 else "dve",
                 (lambda ps=ps, cg=cg: nc.scalar.copy(out=vs[0:nk, cg * 512:(cg + 1) * 512], in_=ps[0:nk, :])) if cg % 2
                 else (lambda ps=ps, cg=cg: nc.vector.tensor_copy(out=vs[0:nk, cg * 512:(cg + 1) * 512], in_=ps[0:nk, :])),
                 reads=[pk], writes=[("vs", cg)])
        T.dma("sp", kv["v"][ksl, :], vs[0:nk, :], reads=[("vs", g) for g in range(4)], writes=[kk])

    def cache_prep(self, l):
        T = self.T
        st2 = self.a1[:, 0:2 * 2624].rearrange("p (b c) -> p b c", b=2)
        self.load_wkv(l)
        i = 0
        for s in (1, 2):
            for kb in range(PAST // 128):
                b = i % 2
                i += 1
                rs = slice(kb * 128, (kb + 1) * 128)
                key = ("cst", b)
                T.dma("pool", st2[:, b, 0:512], self.c_lat[l, s - 1, rs, :], writes=[key])
                T.dma("pool", st2[:, b, 512:576], self.c_kr[l, s - 1, rs, :], writes=[key])
                T.dma("pool", st2[:, b, 576:1600], self.c_sbk[l, s - 1, rs, :], writes=[key])
                T.dma("pool", st2[:, b, 1600:2624], self.c_sbv[l, s - 1, rs, :], writes=[key])
                self.kprep(l, s, kb * 128, 128, st2[:, b, 0:512], st2[:, b, 512:576], st2[:, b, 576:1600],
                           st2[:, b, 1600:2624], [key])
        self.release(self.wuk_k)
        self.release(self.wuv_k)

    def stage_B(self, l, segs, pos0, sample):
        T, nc = self.T, self.nc
        self.oT = self.a1[:, 0:24 * TOK].rearrange("p (c t) -> p c t", c=24)
        sources = sorted(set(s[0] for s in segs))
        for h in range(SBH):
            for src in sources:
                ssegs = [s for s in segs if s[0] == src]
                nblk = max(b[0] for s in ssegs for b in s[3]) + 1
                kv = self.kv[(l, src)]
                buf, bk = self.wbuf()
                kT = buf[:, 0:4096]
                vv = buf[:, 4096:4096 + 32 * 128].rearrange("p (b d) -> p b d", b=32)
                nkeys = sum(b[1] for b in max(ssegs, key=lambda s: len(s[3]))[3])
                T.dma("sp", kT[:, 0:nkeys], kv["sbkT"][h, :, 0:nkeys], reads=[("kv", l, src)], writes=[bk])
                nfull = nkeys // 128
                if nfull:
                    T.dma("sp", vv[:, 0:nfull, :], kv["sbv"][0:nfull * 128, h * 128:(h + 1) * 128].rearrange(
                        "(b p) d -> p b d", p=128), reads=[("kv", l, src)], writes=[bk])
                if nkeys % 128:
                    r = nkeys % 128
                    T.dma("sp", vv[0:r, nfull, :], kv["sbv"][nfull * 128:nkeys, h * 128:(h + 1) * 128],
                          reads=[("kv", l, src)], writes=[bk])
                for (_, c0, nq, blocks) in ssegs:
                    self.sb_segment(h, c0, nq, blocks, kT, vv, bk)
        krbuf, krk = self.wbuf(hold=True)
        wqk = None
        wq = self.w["w_q_b"][l]
        for src in sources:
            ssegs = [s for s in segs if s[0] == src]
            kv = self.kv[(l, src)]
            nkeys = sum(b[1] for b in max(ssegs, key=lambda s: len(s[3]))[3])
            krT = krbuf[0:64, 0:4096]
            T.dma("sp", krT[:, 0:nkeys], kv["krT"][:, 0:nkeys], reads=[("kv", l, src)], writes=[krk])
            wqv = None
            for h in range(MH):
                if h % 8 == 0:
                    if wqk is not None:
                        self.release(wqk)
                    wbuf_, wqk = self.wbuf(hold=True)
                    wqv = wbuf_[:, 0:4 * 1536].rearrange("p (k c) -> p k c", k=4)
                    T.dma("pool", wqv, wq[:, (h // 8) * 1536:(h // 8 + 1) * 1536].rearrange("(k p) c -> p k c", p=128),
                          writes=[wqk])
                buf, bk = self.wbuf()
                kT = buf[:, 0:4096]
                va = buf[:, 4096:4096 + 32 * 129].rearrange("p (b d) -> p b d", b=32)
                T.dma("sp", kT[:, 0:nkeys], kv["knT"][h, :, 0:nkeys], reads=[("kv", l, src)], writes=[bk])
                nfull = nkeys // 128
                if nfull:
                    T.dma("sp", va[:, 0:nfull, 0:128], kv["v"][0:nfull * 128, h * 128:(h + 1) * 128].rearrange(
                        "(b p) d -> p b d", p=128), reads=[("kv", l, src)], writes=[bk])
                if nkeys % 128:
                    r = nkeys % 128
                    T.dma("sp", va[0:r, nfull, 0:128], kv["v"][nfull * 128:nkeys, h * 128:(h + 1) * 128],
                          reads=[("kv", l, src)], writes=[bk])
                T.op("dve", lambda va=va: nc.vector.memset(va[:, :, 128:129], 1.0), writes=[bk])
                cmin = min(s[1] for s in ssegs)
                cmax = max(s[1] + s[2] for s in ssegs)
                qb = h % 2
                self.make_q(h, wqv, wqk, qb, cmin, cmax)
                for (_, c0, nq, blocks) in ssegs:
                    self.mla_segment(h, qb, c0, nq, blocks, kT, krT, va, bk, krk)
            self.release(wqk)
            wqk = None
        self.release(krk)

    def make_q(self, h, wqv, wqk, qb, c0, c1):
        T, nc = self.T, self.nc
        n = c1 - c0
        hb = (h % 8) * 192
        qn = self.qh[:, qb, 0, :]
        qr = self.qh[0:64, qb, 1, :]
        wr = self.wrot[:, qb, :, :]
        kq = ("qh", qb)
        T.op("dve", lambda: nc.vector.tensor_scalar(out=wr[:, :, 0:32], in0=wqv[:, :, hb + 160:hb + 192], scalar1=-1.0,
                                                    scalar2=None, op0=ALU.mult), reads=[wqk], writes=[("wrot", qb)])
        T.op("dve", lambda: nc.vector.tensor_copy(out=wr[:, :, 32:64], in_=wqv[:, :, hb + 128:hb + 160]),
             reads=[wqk], writes=[("wrot", qb)])
        ps, pk = self.psum()
        for c in range(4):
            T.op("pe", lambda c=c: nc.tensor.matmul(ps[:, 0:n], wqv[:, c, hb:hb + 128], self.q_aT[:, c, c0:c1],
                                                    start=(c == 0), stop=(c == 3)),
                 reads=[wqk, ("q_aT", "all")], writes=[pk])
        T.op("act", lambda: nc.scalar.copy(out=qn[:, c0:c1], in_=ps[:, 0:n]), reads=[pk], writes=[kq])
        ps1, pk1 = self.psum()
        ps2, pk2 = self.psum()
        for c in range(4):
            T.op("pe", lambda c=c: nc.tensor.matmul(ps1[0:64, 0:n], wqv[:, c, hb + 128:hb + 192], self.q_aT[:, c, c0:c1],
                                                    start=(c == 0), stop=(c == 3)),
                 reads=[wqk, ("q_aT", "all")], writes=[pk1])
        for c in range(4):
            T.op("pe", lambda c=c: nc.tensor.matmul(ps2[0:64, 0:n], wr[:, c, :], self.q_aT[:, c, c0:c1],
                                                    start=(c == 0), stop=(c == 3)),
                 reads=[("wrot", qb), ("q_aT", "all")], writes=[pk2])
        t1 = self.tmp[0:64, 0:512]
        t2 = self.tmp[0:64, 512:1024]
        T.op("dve", lambda: nc.vector.tensor_tensor(out=t1[:, 0:n], in0=ps1[0:64, 0:n], in1=self.rope2[:, 0, c0:c1],
                                                    op=ALU.mult), reads=[pk1, ("rope2",)], writes=[("R0",)])
        T.op("dve", lambda: nc.vector.tensor_tensor(out=t2[:, 0:n], in0=ps2[0:64, 0:n], in1=self.rope2[:, 1, c0:c1],
                                                    op=ALU.mult), reads=[pk2, ("rope2",)], writes=[("R1",)])
        T.op("dve", lambda: nc.vector.tensor_tensor(out=qr[:, c0:c1], in0=t1[:, 0:n], in1=t2[:, 0:n], op=ALU.add),
             reads=[("R0",), ("R1",)], writes=[kq])

    def mla_segment(self, h, qb, c0, nq, blocks, kT, krT, va, bk, krk):
        T, nc = self.T, self.nc
        qn = self.qh[:, qb, 0, c0:c0 + nq]
        qr = self.qh[0:64, qb, 1, c0:c0 + nq]
        kq = ("qh", qb)
        o_ps, o_k = self.psum(hold=True)
        nb = len(blocks)
        gi = 0
        key_off = 0
        offs = []
        for (blk, nk, m) in blocks:
            offs.append(key_off)
            key_off += nk
        for g0 in range(0, nb, 4):
            grp = blocks[g0:g0 + 4]
            s_ps, s_k = self.psum()
            sv = s_ps[:].rearrange("p (b q) -> p b q", b=4)
            for i, (blk, nk, m) in enumerate(grp):
                ko = offs[g0 + i]
                T.op("pe", lambda i=i, nk=nk, ko=ko: nc.tensor.matmul(sv[0:nk, i, 0:nq], kT[:, ko:ko + nk], qn,
                                                                       start=True, stop=False),
                     reads=[bk, kq], writes=[s_k])
                T.op("pe", lambda i=i, nk=nk, ko=ko: nc.tensor.matmul(sv[0:nk, i, 0:nq], krT[:, ko:ko + nk], qr,
                                                                       start=False, stop=True),
                     reads=[krk, kq], writes=[s_k])
            pi = gi % 3
            gi += 1
            pt = self.pt[:, pi, :].rearrange("p (b q) -> p b q", b=4)
            pkey = ("pt", pi)
            full = [i for i, b in enumerate(grp) if b[1] == 128]
            part = [i for i, b in enumerate(grp) if b[1] != 128]
            if full:
                nf = len(full)
                T.op("act", lambda nf=nf: nc.scalar.activation(out=pt[:, 0:nf, 0:nq], in_=sv[:, 0:nf, 0:nq], func=AF.Exp,
                                                               scale=MLA_SCALE), reads=[s_k], writes=[pkey])
            for i in part:
                nk = grp[i][1]
                T.op("act", lambda i=i, nk=nk: nc.scalar.activation(out=pt[0:nk, i, 0:nq], in_=sv[0:nk, i, 0:nq],
                                                                    func=AF.Exp, scale=MLA_SCALE),
                     reads=[s_k], writes=[pkey])
            for i, (blk, nk, m) in enumerate(grp):
                if m is not None:
                    T.op("dve", lambda i=i, nk=nk, m=m: nc.vector.tensor_tensor(
                        out=pt[0:nk, i, 0:nq], in0=pt[0:nk, i, 0:nq], in1=self.masks[0:nk, m, 0:nq], op=ALU.mult),
                        reads=[pkey, ("masks",)], writes=[pkey])
            for i, (blk, nk, m) in enumerate(grp):
                bi = g0 + i
                T.op("pe", lambda i=i, nk=nk, bi=bi, blk=blk: nc.tensor.matmul(
                    o_ps[0:nq, 0:129], pt[0:nk, i, 0:nq], va[0:nk, blk, :], start=(bi == 0), stop=(bi == nb - 1)),
                    reads=[pkey, bk], writes=[o_k])
        rc = self.small[0:nq, 40:41]
        T.op("dve", lambda: nc.vector.reciprocal(out=rc, in_=o_ps[0:nq, 128:129]), reads=[o_k], writes=[("rc",)])
        ob = self.tmp[:, 5696:5760].bitcast(BF16)
        T.op("dve", lambda: nc.vector.tensor_scalar(out=ob[0:nq, :], in0=o_ps[0:nq, 0:128], scalar1=rc, scalar2=None,
                                                    op0=ALU.mult), reads=[o_k, ("rc",)], writes=[("ob",)])
        self.release(o_k)
        self.transpose_to(ob[0:nq, :], nq, 128, self.oT[:, h, c0:c0 + nq], [("ob",)], [("oT", h)])

    def sb_segment(self, h, c0, nq, blocks, kT, vv, bk):
        T, nc = self.T, self.nc
        q = self.sbqT[:, h, c0:c0 + nq]
        o_ps, o_k = self.psum(hold=True)
        nb = len(blocks)
        offs = []
        ko = 0
        for (blk, nk, m) in blocks:
            offs.append(ko)
            ko += nk
        cs = self.tmp[:, 5760:5888]
        e_t = self.tmp[:, 5888:6016]
        l_t = self.tmp[:, 6016:6144]
        r_bf = self.tmp[:, 2816:2880].bitcast(BF16)
        first = True
        for bi in range(nb - 1, -1, -1):
            blk, nk, m = blocks[bi]
            ko = offs[bi]
            z_ps, z_k = self.psum()
            T.op("pe", lambda nk=nk, ko=ko: nc.tensor.matmul(z_ps[0:nk, 0:nq], kT[:, ko:ko + nk], q, start=True, stop=True),
                 reads=[bk, ("sbqT", "all")], writes=[z_k])
            T.op("act", lambda nk=nk: nc.scalar.activation(out=e_t[0:nk, 0:nq], in_=z_ps[0:nk, 0:nq], func=AF.Exp,
                                                           scale=-SB_SCALE), reads=[z_k], writes=[("sbe",)])
            T.op("act", lambda nk=nk: nc.scalar.activation(out=l_t[0:nk, 0:nq], in_=e_t[0:nk, 0:nq], func=AF.Ln, bias=1.0),
                 reads=[("sbe",)], writes=[("sbl",)])
            T.op("dve", lambda nk=nk: nc.vector.scalar_tensor_tensor(out=r_bf[0:nk, 0:nq], in0=z_ps[0:nk, 0:nq],
                                                                     scalar=SB_SCALE, in1=l_t[0:nk, 0:nq],
                                                                     op0=ALU.mult, op1=ALU.add),
                 reads=[z_k, ("sbl",)], writes=[("sbr",)])
            if m is not None:
                T.op("dve", lambda nk=nk, m=m: nc.vector.tensor_tensor(out=r_bf[0:nk, 0:nq], in0=r_bf[0:nk, 0:nq],
                                                                       in1=self.masks[0:nk, m, 0:nq], op=ALU.mult),
                     reads=[("sbr",), ("masks",)], writes=[("sbr",)])
            t_ps, t_k = self.psum()
            T.op("pe", lambda nk=nk: nc.tensor.matmul(t_ps[0:nk, 0:nq], self.masks[0:nk, 2, 0:nk], r_bf[0:nk, 0:nq],
                                                      start=True, stop=True), reads=[("sbr",), ("masks",)], writes=[t_k])
            T.op("pe", lambda nk=nk: nc.tensor.matmul(t_ps[:, 128:128 + nq], self.ones[0:nk, :], r_bf[0:nk, 0:nq],
                                                      start=True, stop=True), reads=[("sbr",), ("ones",)], writes=[t_k])
            T.op("dve", lambda nk=nk: nc.vector.tensor_tensor(out=e_t[0:nk, 0:nq], in0=t_ps[0:nk, 0:nq], in1=l_t[0:nk, 0:nq],
                                                              op=ALU.add), reads=[t_k, ("sbl",), ("sbe",)], writes=[("sbe",)])
            if not first:
                T.op("dve", lambda nk=nk: nc.vector.tensor_tensor(out=e_t[0:nk, 0:nq], in0=e_t[0:nk, 0:nq],
                                                                  in1=cs[0:nk, 0:nq], op=ALU.add),
                     reads=[("sbe",), ("sbcs",)], writes=[("sbe",)])
            a_bf = self.pt[:, bi % 3, 0:128]
            akey = ("pt", bi % 3)
            T.op("act", lambda nk=nk, a_bf=a_bf: nc.scalar.activation(out=a_bf[0:nk, 0:nq], in_=e_t[0:nk, 0:nq], func=AF.Exp,
                                                                      scale=-1.0), reads=[("sbe",)], writes=[akey])
            if m is not None:
                T.op("dve", lambda nk=nk, m=m, a_bf=a_bf: nc.vector.tensor_tensor(
                    out=a_bf[0:nk, 0:nq], in0=a_bf[0:nk, 0:nq], in1=self.masks[0:nk, m, 0:nq], op=ALU.mult),
                    reads=[akey, ("masks",)], writes=[akey])
            if bi > 0:
                if first:
                    T.op("dve", lambda: nc.vector.tensor_copy(out=cs[:, 0:nq], in_=t_ps[:, 128:128 + nq]),
                         reads=[t_k], writes=[("sbcs",)])
                else:
                    T.op("dve", lambda: nc.vector.tensor_tensor(out=cs[:, 0:nq], in0=cs[:, 0:nq], in1=t_ps[:, 128:128 + nq],
                                                                op=ALU.add), reads=[t_k, ("sbcs",)], writes=[("sbcs",)])
            T.op("pe", lambda nk=nk, blk=blk, a_bf=a_bf, bi=bi: nc.tensor.matmul(
                o_ps[0:nq, 0:128], a_bf[0:nk, 0:nq], vv[0:nk, blk, :], start=(bi == nb - 1), stop=(bi == 0)),
                reads=[akey, bk], writes=[o_k])
            first = False
        ob = self.tmp[:, 5696:5760].bitcast(BF16)
        T.op("act", lambda: nc.scalar.copy(out=ob[0:nq, :], in_=o_ps[0:nq, 0:128]), reads=[o_k], writes=[("ob",)])
        self.release(o_k)
        self.transpose_to(ob[0:nq, :], nq, 128, self.oT[:, 16 + h, c0:c0 + nq], [("ob",)], [("oT", 16 + h)], evac="dve")

    def stage_C(self, l, rows_list):
        T, nc = self.T, self.nc
        ntl = len(rows_list)
        ntok = sum(rows_list)
        self.mT = self.a3[:, 0:16 * TOK].rearrange("p (c t) -> p c t", c=16)
        wa, wbb, win = self.w["w_branch_a"][l], self.w["w_branch_b"][l], self.w["w_in"][l]
        for a in range(2):
            for c in range(16):
                T.dma("sp", self.bcol[:, a, c:c + 1],
                      self.w["b_gate"][l][(a * 16 + c) * 128:(a * 16 + c + 1) * 128].rearrange("(p o) -> p o", o=1),
                      writes=[("bcol",)])
        ga = self.tmp[:, 0:512]
        gb = self.tmp[:, 512:1024]
        t1 = self.tmp[:, 1024:1536]
        for c in range(16):
            buf, bk = self.wbuf()
            wv = buf[:, 0:56 * 128].rearrange("p (k c) -> p k c", k=56)
            cs_ = slice(c * 128, (c + 1) * 128)
            T.dma("pool", wv[:, 0:16, :], wa[:, cs_].rearrange("(k p) c -> p k c", p=128), writes=[bk])
            T.dma("pool", wv[:, 16:24, :], wbb[:, cs_].rearrange("(k p) c -> p k c", p=128), writes=[bk])
            T.dma("pool", wv[:, 24:40, :], win[:, 4160 + c * 128:4160 + (c + 1) * 128].rearrange("(k p) c -> p k c", p=128),
                  writes=[bk])
            T.dma("pool", wv[:, 40:56, :], win[:, 6208 + c * 128:6208 + (c + 1) * 128].rearrange("(k p) c -> p k c", p=128),
                  writes=[bk])
            pa, ka = self.psum()
            pb, kb = self.psum()
            pga, kga = self.psum()
            pgb, kgb = self.psum()
            for k in range(16):
                T.op("pe", lambda k=k: nc.tensor.matmul(pa[:, 0:ntok], wv[:, k, :], self.oT[:, k, 0:ntok],
                                                        start=(k == 0), stop=(k == 15)), reads=[bk, ("oT", k)], writes=[ka])
            for k in range(8):
                T.op("pe", lambda k=k: nc.tensor.matmul(pb[:, 0:ntok], wv[:, 16 + k, :], self.oT[:, 16 + k, 0:ntok],
                                                        start=(k == 0), stop=(k == 7)), reads=[bk, ("oT", 16 + k)], writes=[kb])
            for k in range(16):
                T.op("pe", lambda k=k: nc.tensor.matmul(pga[:, 0:ntok], wv[:, 24 + k, :], self.xT[:, k, 0:ntok],
                                                        start=(k == 0), stop=(k == 15)), reads=[bk, ("xT", "all")], writes=[kga])
            for k in range(16):
                T.op("pe", lambda k=k: nc.tensor.matmul(pgb[:, 0:ntok], wv[:, 40 + k, :], self.xT[:, k, 0:ntok],
                                                        start=(k == 0), stop=(k == 15)), reads=[bk, ("xT", "all")], writes=[kgb])
            T.op("act", lambda c=c: nc.scalar.activation(out=ga[:, 0:ntok], in_=pga[:, 0:ntok], func=AF.Sigmoid,
                                                         bias=self.bcol[:, 0, c:c + 1]), reads=[kga, ("bcol",)], writes=[("R0",)])
            T.op("act", lambda c=c: nc.scalar.activation(out=gb[:, 0:ntok], in_=pgb[:, 0:ntok], func=AF.Sigmoid,
                                                         bias=self.bcol[:, 1, c:c + 1]), reads=[kgb, ("bcol",)], writes=[("R1",)])
            T.op("dve", lambda: nc.vector.tensor_tensor(out=ga[:, 0:ntok], in0=ga[:, 0:ntok], in1=pa[:, 0:ntok], op=ALU.mult),
                 reads=[("R0",), ka], writes=[("R0",)])
            T.op("dve", lambda: nc.vector.tensor_tensor(out=gb[:, 0:ntok], in0=gb[:, 0:ntok], in1=pb[:, 0:ntok], op=ALU.mult),
                 reads=[("R1",), kb], writes=[("R1",)])
            T.op("dve", lambda c=c: nc.vector.tensor_tensor(out=self.mT[:, c, 0:ntok], in0=ga[:, 0:ntok], in1=gb[:, 0:ntok],
                                                            op=ALU.add), reads=[("R0",), ("R1",)], writes=[("mT", c)])
        wo = self.w["w_out"][l]
        for cg in range(4):
            wv, wk = self.load_w(wo[:, cg * 512:(cg + 1) * 512], 16, 512)
            for t in range(ntl):
                rows = rows_list[t]
                ps, pk = self.psum()
                for k in range(16):
                    T.op("pe", lambda k=k, t=t, rows=rows, ps=ps, wv=wv: nc.tensor.matmul(
                        ps[0:rows, :], self.mT[:, k, t * 128:t * 128 + rows], wv[:, k, :], start=(k == 0), stop=(k == 15)),
                        reads=[("mT", k), wk], writes=[pk])
                xs_ = self.x[0:rows, t, cg * 512:(cg + 1) * 512]
                T.op("dve", lambda xs_=xs_, ps=ps, rows=rows: nc.vector.scalar_tensor_tensor(
                    out=xs_, in0=xs_, scalar=ALPHA, in1=ps[0:rows, :], op0=ALU.mult, op1=ALU.add),
                    reads=[("x", t), pk], writes=[("x", t)])
        self.load_lnw("ln1_g", "ln1_b", l)
        for t in range(ntl):
            self.layer_norm(t, rows_list[t], "ln1_g", "ln1_b", l)

    def stage_D(self, l, rows_list):
        T, nc = self.T, self.nc
        ntl = len(rows_list)
        ntok = sum(rows_list)
        wgu, wdn = self.w["w_ffn_gu"][l], self.w["w_ffn_down"][l]
        hT = self.a3[:, 0:4 * TOK].rearrange("p (b s t) -> p b s t", b=2, s=2)
        sg = self.tmp[:, 0:512]
        FC = 256
        nchunk = FF // FC
        for j in range(nchunk):
            buf, bk = self.wbuf()
            wg = buf[:, 0:16 * 512].rearrange("p (k c) -> p k c", k=16)
            T.dma("pool", wg[:, :, 0:256], wgu[:, j * FC:(j + 1) * FC].rearrange("(k p) c -> p k c", p=128), writes=[bk])
            T.dma("pool", wg[:, :, 256:512], wgu[:, FF + j * FC:FF + (j + 1) * FC].rearrange("(k p) c -> p k c", p=128),
                  writes=[bk])
            buf2, bk2 = self.wbuf()
            wd = buf2[:, 0:2 * 2048].rearrange("p (s c) -> p s c", s=2)
            T.dma("pool", wd, wdn[j * FC:(j + 1) * FC, :].rearrange("(s p) c -> p s c", p=128), writes=[bk2])
            hb = j % 2
            for s in range(2):
                pg, kg = self.psum()
                pu, ku = self.psum()
                for k in range(16):
                    T.op("pe", lambda k=k, s=s, pg=pg: nc.tensor.matmul(pg[:, 0:ntok], wg[:, k, s * 128:(s + 1) * 128],
                                                                        self.xT[:, k, 0:ntok], start=(k == 0), stop=(k == 15)),
                         reads=[bk, ("xT", "all")], writes=[kg])
                for k in range(16):
                    T.op("pe", lambda k=k, s=s, pu=pu: nc.tensor.matmul(pu[:, 0:ntok], wg[:, k, 256 + s * 128:256 + (s + 1) * 128],
                                                                        self.xT[:, k, 0:ntok], start=(k == 0), stop=(k == 15)),
                         reads=[bk, ("xT", "all")], writes=[ku])
                T.op("act", lambda pg=pg: nc.scalar.activation(out=sg[:, 0:ntok], in_=pg[:, 0:ntok], func=AF.Silu),
                     reads=[kg], writes=[("R0",)])
                T.op("dve", lambda pu=pu, s=s, hb=hb: nc.vector.tensor_tensor(out=hT[:, hb, s, 0:ntok], in0=sg[:, 0:ntok],
                                                                              in1=pu[:, 0:ntok], op=ALU.mult),
                     reads=[("R0",), ku], writes=[("hT", hb)])
            for t in range(ntl):
                rows = rows_list[t]
                for cg in range(4):
                    ps, pk = self.psum()
                    for s in range(2):
                        T.op("pe", lambda s=s, t=t, rows=rows, ps=ps, cg=cg, hb=hb: nc.tensor.matmul(
                            ps[0:rows, :], hT[:, hb, s, t * 128:t * 128 + rows], wd[:, s, cg * 512:(cg + 1) * 512],
                            start=(s == 0), stop=(s == 1)), reads=[("hT", hb), bk2], writes=[pk])
                    xs_ = self.x[0:rows, t, cg * 512:(cg + 1) * 512]
                    if j == 0:
                        T.op("dve", lambda xs_=xs_, ps=ps, rows=rows: nc.vector.scalar_tensor_tensor(
                            out=xs_, in0=xs_, scalar=ALPHA, in1=ps[0:rows, :], op0=ALU.mult, op1=ALU.add),
                            reads=[("x", t), pk], writes=[("x", t)])
                    else:
                        T.op("dve", lambda xs_=xs_, ps=ps, rows=rows: nc.vector.tensor_tensor(
                            out=xs_, in0=xs_, in1=ps[0:rows, :], op=ALU.add), reads=[("x", t), pk], writes=[("x", t)])
        self.load_lnw("ln2_g", "ln2_b", l)
        for t in range(ntl):
            self.layer_norm(t, rows_list[t], "ln2_g", "ln2_b", l)

    def stage_E(self, l, rows_list, p_src, last, y_dst):
        T, nc = self.T, self.nc
        ntl = len(rows_list)
        pT = self.a3[:, 0:2 * TOK].rearrange("p (c t) -> p c t", c=2)
        pb = self.a3[:, 2 * TOK:2 * TOK + NT * 256].rearrange("p (t c) -> p t c", t=NT)
        for t in range(ntl):
            rows = rows_list[t]
            T.dma("pool", pb[0:rows, t, :], p_src[t * 128:t * 128 + rows, :], writes=[("pb", t)])
            self.transposes4([pb[0:rows, t, i * 128:(i + 1) * 128] for i in range(2)], rows,
                             pT[:, :, t * 128:t * 128 + rows], [("pb", t)], [("pT", t)])
        bbuf, bbk = self.wbuf(hold=True)
        bias = bbuf[:, 0:4096].bitcast(F32)
        T.dma("sp", bias, self.w["b_ple_gate"][l].partition_broadcast(128), writes=[bbk])
        wg, wp = self.w["w_ple_gate"][l], self.w["w_ple_proj"][l]
        wpv = bbuf[:, 4096:8192].rearrange("p (k c) -> p k c", k=2)
        T.dma("pool", wpv, wp.rearrange("(k p) c -> p k c", p=128), writes=[bbk])
        g1 = self.tmp[:, 0:512]
        for cg in range(4):
            buf, bk = self.wbuf()
            wv = buf[:, 0:16 * 512].rearrange("p (k c) -> p k c", k=16)
            T.dma("pool", wv[:, 0:16, :], wg[:, cg * 512:(cg + 1) * 512].rearrange("(k p) c -> p k c", p=128), writes=[bk])
            for t in range(ntl):
                rows = rows_list[t]
                pg, kg = self.psum()
                pp_, kp = self.psum()
                for k in range(16):
                    T.op("pe", lambda k=k, t=t, rows=rows, pg=pg: nc.tensor.matmul(
                        pg[0:rows, :], self.xT[:, k, t * 128:t * 128 + rows], wv[:, k, :], start=(k == 0), stop=(k == 15)),
                        reads=[("xT", t), bk], writes=[kg])
                for k in range(2):
                    T.op("pe", lambda k=k, t=t, rows=rows, pp_=pp_: nc.tensor.matmul(
                        pp_[0:rows, :], pT[:, k, t * 128:t * 128 + rows], wpv[:, k, cg * 512:(cg + 1) * 512],
                        start=(k == 0), stop=(k == 1)), reads=[("pT", t), bbk], writes=[kp])
                T.op("dve", lambda rows=rows, pg=pg, cg=cg: nc.vector.tensor_tensor(
                    out=g1[0:rows, :], in0=pg[0:rows, :], in1=bias[0:rows, cg * 512:(cg + 1) * 512], op=ALU.add),
                    reads=[kg, bbk], writes=[("R0",)])
                T.op("act", lambda rows=rows: nc.scalar.activation(out=g1[0:rows, :], in_=g1[0:rows, :], func=AF.Sigmoid),
                     reads=[("R0",)], writes=[("R0",)])
                T.op("dve", lambda rows=rows, pp_=pp_: nc.vector.tensor_tensor(out=g1[0:rows, :], in0=g1[0:rows, :],
                                                                               in1=pp_[0:rows, :], op=ALU.mult),
                     reads=[("R0",), kp], writes=[("R0",)])
                xs_ = self.x[0:rows, t, cg * 512:(cg + 1) * 512]
                T.op("dve", lambda xs_=xs_, rows=rows: nc.vector.scalar_tensor_tensor(
                    out=xs_, in0=xs_, scalar=ALPHA, in1=g1[0:rows, :], op0=ALU.mult, op1=ALU.add),
                    reads=[("x", t), ("R0",)], writes=[("x", t)])
        self.release(bbk)
        self.load_lnw("ln3_g", "ln3_b", l)
        for t in range(ntl):
            rows = rows_list[t]
            self.layer_norm(t, rows, "ln3_g", "ln3_b", l, make_xT=not last,
                            out_dram=(y_dst[t * 128:t * 128 + rows, :] if last else None))

    def run_pass(self, g, sample):
        T, nc = self.T, self.nc
        if sample:
            rows_list = [64]
            x_src, y_dst = self.xs, self.y_s
            tab0 = SEQ
        else:
            rows_list = [128] * NT
            x_src, y_dst = self.xp[g * TOK:(g + 1) * TOK, :], self.y_p[g * TOK:(g + 1) * TOK, :]
            tab0 = g * TOK
        ntl = len(rows_list)
        ntok = sum(rows_list)
        T.barrier()
        for t in range(ntl):
            rows = rows_list[t]
            T.dma("sp", self.ropeT[0:rows, t, 0, :], self.cosT[tab0 + t * 128:tab0 + t * 128 + rows, :], writes=[("ropeT",)])
            T.dma("sp", self.ropeT[0:rows, t, 1, :], self.sinT[tab0 + t * 128:tab0 + t * 128 + rows, :], writes=[("ropeT",)])
        T.dma("sp", self.rope2[:, 0, 0:ntok], self.cos2[:, tab0:tab0 + ntok], writes=[("rope2",)])
        T.dma("sp", self.rope2[:, 1, 0:ntok], self.sin2[:, tab0:tab0 + ntok], writes=[("rope2",)])
        self.load_lnw("ln_in_g", "ln_in_b", None)
        for t in range(ntl):
            rows = rows_list[t]
            T.dma("sp", self.x[0:rows, t, :], x_src[t * 128:t * 128 + rows, :], writes=[("x", t)])
            self.layer_norm(t, rows, "ln_in_g", "ln_in_b", None)
        import os
        STOP = int(os.environ.get("KSTOP", "99"))
        if STOP <= 0:
            return
        for l in range(DEPTH):
            if STOP < 90 and l > 0:
                return
            T.barrier()
            if sample:
                self.cache_prep(l)
                T.barrier()
            self.stage_A(l, rows_list, None, g * TOK, sample)
            if STOP <= 1:
                return
            T.barrier()
            if sample:
                segs = []
                for s in (1, 2):
                    blocks_m = [(b, 128, None) for b in range(16)] + [(16, 32, None)]
                    segs.append((s, (s - 1) * 32, 32, blocks_m))
                self.seg_masks = {"mla": None, "sb": 1}
            else:
                segs = []
                for i in range(NT):
                    p = g * NT + i
                    segs.append((0, i * 128, 128, [(b, 128, None) for b in range(p)] + [(p, 128, "diag")]))
            self.stage_B_wrapper(l, segs, sample)
            if STOP <= 2:
                return
            T.barrier()
            self.stage_C(l, rows_list)
            if STOP <= 3:
                return
            T.barrier()
            self.stage_D(l, rows_list)
            if STOP <= 4:
                return
            T.barrier()
            p_src = (self.psm[l] if sample else self.pp[l, g * TOK:(g + 1) * TOK, :])
            self.stage_E(l, rows_list, p_src, l == DEPTH - 1, y_dst)

    def stage_B_wrapper(self, l, segs, sample):
        self._segs = segs
        self._sample = sample
        orig_mla, orig_sb = self.mla_segment, self.sb_segment

        def mla(h, qb, c0, nq, blocks, kT, krT, va, bk, krk):
            bl = [(b, nk, (0 if m == "diag" else None)) for (b, nk, m) in blocks]
            return orig_mla(h, qb, c0, nq, bl, kT, krT, va, bk, krk)

        def sbs(h, c0, nq, blocks, kT, vv, bk):
            if sample:
                bl = [(b, nk, (1 if nk == 32 else None)) for (b, nk, m) in blocks]
            else:
                bl = [(b, nk, (1 if m == "diag" else None)) for (b, nk, m) in blocks]
            return orig_sb(h, c0, nq, bl, kT, vv, bk)

        self.mla_segment, self.sb_segment = mla, sbs
        try:
            self.stage_B(l, segs, 0, sample)
        finally:
            self.mla_segment, self.sb_segment = orig_mla, orig_sb

    def build(self, passes):
        from contextlib import ExitStack
        with ExitStack() as es:
            self.es = es
            self.T.setup(es)
            self.declare()
            self.bounce = self.scratch("bounce", [128, 2624])
            self.alloc()
            self.consts()
            for (g, sample) in passes:
                self.run_pass(g, sample)
            self.T.barrier()
            self.T.finish()


PASSES = [(g, False) for g in range(NPASS)] + [(0, True)]


def build_nc(passes=PASSES):
    nc0 = bass.Bass("TRN2", target_bir_lowering=False)
    t0 = Tracker(nc0, None)
    Builder(nc0, t0).build(passes)
    plan = t0.make_plan()
    nc = bass.Bass("TRN2", target_bir_lowering=False)
    t1 = Tracker(nc, plan)
    Builder(nc, t1).build(passes)
    assert t1.n == plan["n"], (t1.n, plan["n"])
    return nc


def _tables():
    half = ROPE // 2
    inv_freq = (1.0 / (np.float32(10000.0) ** (np.arange(half, dtype=np.float32) * np.float32(2.0 / ROPE)))).astype(np.float32)
    pos = np.concatenate([np.arange(SEQ), PAST + np.arange(32), PAST + np.arange(32)]).astype(np.float32)
    ang = (pos[:, None] * inv_freq[None, :]).astype(np.float32)
    cos, sin = np.cos(ang).astype(np.float32), np.sin(ang).astype(np.float32)
    cos2 = np.ascontiguousarray(np.concatenate([cos, cos], 1).T)
    sin2 = np.ascontiguousarray(np.concatenate([sin, sin], 1).T)
    k = np.arange(128)[:, None]
    q = np.arange(128)[None, :]
    m = np.zeros((128, 3, 128), np.float32)
    m[:, 0, :] = (k // 64) <= (q // 64)
    m[:, 1, :] = k < q
    m[:, 2, :] = k > q
    ident = np.eye(128, dtype=np.float32)
    return cos, sin, cos2, sin2, m.astype(ml_dtypes.bfloat16), ident.astype(ml_dtypes.bfloat16)


_NC_CACHE = {}


def kernel(**inp):
    f = lambda a: np.ascontiguousarray(np.asarray(a, dtype=np.float32))
    if "nc" not in _NC_CACHE:
        _NC_CACHE["nc"] = build_nc()
    nc = _NC_CACHE["nc"]
    cos, sin, cos2, sin2, cmask, ident = _tables()
    wnames = ["ln_in_g", "ln_in_b", "w_in", "b_gate", "q_a_norm_g", "w_q_b", "kv_a_norm_g", "w_kv_b", "w_branch_a",
              "w_branch_b", "w_out", "ln1_g", "ln1_b", "w_ffn_gu", "w_ffn_down", "ln2_g", "ln2_b", "w_ple_gate",
              "b_ple_gate", "w_ple_proj", "ln3_g", "ln3_b"]
    shared = {n: f(inp[n]) for n in wnames}
    shared.update(cosT=cos, sinT=sin, cos2=cos2, sin2=sin2, cmask=cmask, ident=ident)
    xp, xs = f(inp["x_prompt"]), f(inp["x_sample"])
    cl, ck, csk, csv = f(inp["cache_mla_latent"]), f(inp["cache_mla_krope"]), f(inp["cache_sb_k"]), f(inp["cache_sb_v"])
    pp, psm = f(inp["p_prompt"]), f(inp["p_sample"])
    in_maps = []
    for c in range(8):
        b = c // 4
        sb = slice(2 * c, 2 * c + 2)
        m = dict(shared)
        m["xp"] = xp[b]
        m["xs"] = np.ascontiguousarray(xs[sb].reshape(64, D))
        m["c_lat"] = np.ascontiguousarray(cl[:, sb])
        m["c_kr"] = np.ascontiguousarray(ck[:, sb])
        m["c_sbk"] = np.ascontiguousarray(csk[:, sb].reshape(DEPTH, 2, PAST, SBW))
        m["c_sbv"] = np.ascontiguousarray(csv[:, sb].reshape(DEPTH, 2, PAST, SBW))
        m["pp"] = np.ascontiguousarray(pp[:, b])
        m["psm"] = np.ascontiguousarray(psm[:, sb].reshape(DEPTH, 64, PLE))
        in_maps.append(m)
    import os
    ncores = int(os.environ.get("KCORES", "8"))
    if os.environ.get("KTRACE"):
        res = run_bass_kernel_spmd(nc, in_maps[:ncores], core_ids=list(range(ncores)), trace=True)
        print("EXEC_TIME_NS", res.exec_time_ns)
    else:
        res = run_bass_kernel_spmd(nc, in_maps[:ncores], core_ids=list(range(ncores)))
    R = list(res.results)
    while len(R) < 8:
        R.append(R[0])
    y_p = np.stack([R[0]["y_p"], R[4]["y_p"]]).astype(np.float32)
    y_s = np.concatenate([R[c]["y_s"].reshape(2, 32, D) for c in range(8)], 0).astype(np.float32)
    pc = lambda n, shp: np.stack([R[0][n], R[4][n]], 1).reshape(shp).astype(np.float32)
    lat_p = pc("o_lat", (DEPTH, 2, SEQ, KVL))
    kr_p = pc("o_kr", (DEPTH, 2, SEQ, ROPE))
    k_p = pc("o_sbk", (DEPTH, 2, SEQ, SBH, SBD))
    v_p = pc("o_sbv", (DEPTH, 2, SEQ, SBH, SBD))
    sc = lambda n, shp: np.concatenate([R[c][n].reshape(DEPTH, 2, 32, -1) for c in range(8)], 1).reshape(shp).astype(np.float32)
    lat_s = sc("os_lat", (DEPTH, 16, 32, KVL))
    kr_s = sc("os_kr", (DEPTH, 16, 32, ROPE))
    k_s = sc("os_sbk", (DEPTH, 16, 32, SBH, SBD))
    v_s = sc("os_sbv", (DEPTH, 16, 32, SBH, SBD))
    return (y_p, y_s, lat_p, kr_p, k_p, v_p, lat_s, kr_s, k_s, v_s)
```

```python
import numpy as np
import ml_dtypes
import concourse.bass as bass
import concourse.mybir as mybir
from concourse.bass_utils import run_bass_kernel_spmd

F32 = mybir.dt.float32
BF16 = mybir.dt.bfloat16
AF = mybir.ActivationFunctionType
ALU = mybir.AluOpType
AX = mybir.AxisListType

D = 2048; SEQ = 4096; DEPTH = 2; PAST = 2048; DSEQ = 32
QL = 512; KVL = 512; ROPE = 64; NOPE = 128; VD = 128; MH = 16
SBH = 8; SBD = 128; SBW = 1024; FF = 5632; PLE = 256
INC = 8256
MLA_SCALE = float((NOPE + ROPE) ** -0.5)
SB_SCALE = float(SBD ** -0.5)
ALPHA = float((2 * DEPTH) ** 0.25)
LN_EPS = 1e-5; RMS_EPS = 1e-6
NT = 4
TOK = NT * 128
NPASS = SEQ // TOK
SKEYS = 2176
NWB = 4
WBE = 8448


COMPUTE = ("pe", "act", "dve")


class Tracker:
    def __init__(self, nc, plan=None):
        self.nc = nc
        self.plan = plan
        self.dry = plan is None
        self.n = 0
        self.last_w = {}
        self.readers = {}
        self.deps = []
        self.unit = []
        self.queue = []
        self.last_on = {}
        self.dma_since_bar = []
        self.nobar_ops = set()

    def setup(self, es):
        nc, plan = self.nc, self.plan
        if not self.dry:
            sem = lambda name: es.enter_context(nc.semaphore(name))
            self.sems = {e: sem(f"s_{e}") for e in COMPUTE}
            self.bar = sem("s_bar")
            self.dsems = {q: [sem(f"d_{q}{i}") for i in range(plan["R"])] for q in ("sp", "pool")}
            self.eng = {"pe": nc.tensor, "act": nc.scalar, "dve": nc.vector, "sp": nc.sync, "pool": nc.gpsimd}

    def _record(self, unit, reads, writes, queue):
        i = self.n
        d = set()
        for r in reads:
            w = self.last_w.get(r)
            if w is not None:
                d.add(w)
        for k in writes:
            w = self.last_w.get(k)
            if w is not None:
                d.add(w)
            rd = self.readers.get(k)
            if rd:
                d.update(rd.values())
        for r in reads:
            self.readers.setdefault(r, {})[unit if unit != "dma" else ("dma", i)] = i
        for k in writes:
            self.last_w[k] = i
            self.readers[k] = {}
        d.discard(i)
        self.deps.append(d)
        self.unit.append(unit)
        self.queue.append(queue)
        if unit == "dma":
            self.dma_since_bar.append(i)
        else:
            self.last_on[unit] = i

    def op(self, unit, fn, reads=(), writes=()):
        if self.dry:
            self._record(unit, reads, writes, None)
        else:
            self._emit(fn)
        self.n += 1

    def dma(self, queue, out, in_, reads=(), writes=(), nobar=False):
        if self.dry:
            self.nobar_ops.add(self.n) if nobar else None
            self._record("dma", reads, writes, queue)
        else:
            self._emit(lambda: self.eng[queue].dma_start(out=out, in_=in_))
        self.n += 1

    def barrier(self):
        if self.dry:
            i = self.n
            d = set(self.last_on.values()) | set(self.dma_since_bar)
            self.dma_since_bar = []
            self.deps.append(d)
            self.unit.append("bar")
            self.queue.append("sp")
        else:
            self._emit(None)
        self.n += 1

    def make_plan(self, R=4):
        n = self.n
        unit, queue, deps = self.unit, self.queue, self.deps
        needs_inc = [False] * n
        for i in range(n):
            for j in deps[i]:
                if unit[j] in COMPUTE:
                    if unit[j] == unit[i] and unit[i] == "pe":
                        continue
                    needs_inc[j] = True
        cnt = {e: 0 for e in COMPUTE}
        incidx = [0] * n
        dcount = {"sp": 0, "pool": 0}
        dslot = [None] * n
        nbar = 0
        for i in range(n):
            u = unit[i]
            if u in COMPUTE:
                if needs_inc[i]:
                    cnt[u] += 1
                incidx[i] = cnt[u]
            elif u == "dma":
                q = queue[i]
                k = dcount[q]
                dcount[q] += 1
                dslot[i] = (q, k % R, 16 * (k // R + 1))
        waited = {}
        waits = [None] * n
        for i in range(n):
            u = unit[i]
            stream = u if u in COMPUTE else queue[i]
            need = {}
            for j in deps[i]:
                uj = unit[j]
                if uj in COMPUTE:
                    if uj == u and u == "pe":
                        continue
                    key = ("c", uj)
                    need[key] = max(need.get(key, 0), incidx[j])
                elif uj == "dma":
                    q, s, v = dslot[j]
                    key = ("d", q, s)
                    need[key] = max(need.get(key, 0), v)
            if u == "dma":
                q, s, v = dslot[i]
                if v > 16:
                    key = ("d", q, s)
                    need[key] = max(need.get(key, 0), v - 16)
                if q == "pool" and i not in self.nobar_ops and nbar > 0:
                    need[("bar",)] = nbar
            w = []
            wd = waited.setdefault(stream, {})
            for key, v in need.items():
                if wd.get(key, 0) < v:
                    wd[key] = v
                    w.append((key, v))
            if u == "bar":
                nbar += 1
                waits[i] = (w, nbar)
                for st in COMPUTE:
                    ws = waited.setdefault(st, {})
                    for key, v in wd.items():
                        if ws.get(key, 0) < v:
                            ws[key] = v
            else:
                waits[i] = w
        return {"R": R, "needs_inc": needs_inc, "dslot": dslot, "waits": waits, "unit": unit,
                "queue": queue, "n": n, "dcount": dcount}

    def _sem(self, key):
        if key[0] == "bar":
            return self.bar
        if key[0] == "c":
            return self.sems[key[1]]
        return self.dsems[key[1]][key[2]]

    def _emit(self, fn):
        i = self.n
        p = self.plan
        u = p["unit"][i]
        if u == "bar":
            w, k = p["waits"][i]
            for key, v in w:
                self.nc.sync.wait_ge(self._sem(key), v)
            self.nc.sync.sem_inc(self.bar, 1)
            for e in COMPUTE:
                self.eng[e].wait_ge(self.bar, k)
            return
        stream = u if u in COMPUTE else p["queue"][i]
        eng = self.eng[stream]
        ws = p["waits"][i]
        for key, v in ws[:-1]:
            eng.wait_ge(self._sem(key), v)
        ins = fn()
        if ws:
            key, v = ws[-1]
            ins._wait_ge(self._sem(key), v)
        if u in COMPUTE:
            if p["needs_inc"][i]:
                ins.then_inc(self.sems[u], 1)
        else:
            q, s, v = p["dslot"][i]
            ins.then_inc(self.dsems[q][s], 16)

    def finish(self):
        if self.dry:
            return
        p = self.plan
        R = p["R"]
        for q in ("sp", "pool"):
            k = p["dcount"][q]
            for s in range(R):
                cntq = (k - s + R - 1) // R if k > s else 0
                if cntq > 0:
                    self.nc.sync.wait_ge(self.dsems[q][s], 16 * cntq)


class Builder:
    def __init__(self, nc, T):
        self.nc = nc
        self.T = T
        self.ps_i = 0
        self.wb_i = 0
        self.ps_held = set()
        self.wb_held = set()

    def dram_in(self, name, shape, dt=F32):
        return self.nc.dram_tensor(name, list(shape), dt, kind="ExternalInput").ap()

    def dram_out(self, name, shape):
        return self.nc.dram_tensor(name, list(shape), F32, kind="ExternalOutput").ap()

    def scratch(self, name, shape, dt=BF16):
        return self.nc.dram_tensor(name, list(shape), dt).ap()

    def sb(self, name, shape, dt):
        return self.es.enter_context(self.nc.sbuf_tensor(name, list(shape), dt))

    def psum(self, hold=False):
        while True:
            i = self.ps_i % 8
            self.ps_i += 1
            if i not in self.ps_held:
                break
        if hold:
            self.ps_held.add(i)
        return self.ps[i], ("ps", i)

    def wbuf(self, hold=False):
        while True:
            i = self.wb_i % NWB
            self.wb_i += 1
            if i not in self.wb_held:
                break
        if hold:
            self.wb_held.add(i)
        return self.wb[i], ("wb", i)

    def release(self, key):
        (self.ps_held if key[0] == "ps" else self.wb_held).discard(key[1])

    def declare(self):
        L = DEPTH
        self.xp = self.dram_in("xp", [SEQ, D])
        self.xs = self.dram_in("xs", [64, D])
        self.c_lat = self.dram_in("c_lat", [L, 2, PAST, KVL])
        self.c_kr = self.dram_in("c_kr", [L, 2, PAST, ROPE])
        self.c_sbk = self.dram_in("c_sbk", [L, 2, PAST, SBW])
        self.c_sbv = self.dram_in("c_sbv", [L, 2, PAST, SBW])
        self.pp = self.dram_in("pp", [L, SEQ, PLE])
        self.psm = self.dram_in("psm", [L, 64, PLE])
        self.w = {}
        for name, shape in [("ln_in_g", [D]), ("ln_in_b", [D]), ("w_in", [L, D, INC]), ("b_gate", [L, 2 * D]),
                            ("q_a_norm_g", [L, QL]), ("w_q_b", [L, QL, MH * 192]), ("kv_a_norm_g", [L, KVL]),
                            ("w_kv_b", [L, KVL, MH * 256]), ("w_branch_a", [L, D, D]), ("w_branch_b", [L, SBW, D]),
                            ("w_out", [L, D, D]), ("ln1_g", [L, D]), ("ln1_b", [L, D]), ("w_ffn_gu", [L, D, 2 * FF]),
                            ("w_ffn_down", [L, FF, D]), ("ln2_g", [L, D]), ("ln2_b", [L, D]),
                            ("w_ple_gate", [L, D, D]), ("b_ple_gate", [L, D]), ("w_ple_proj", [L, PLE, D]),
                            ("ln3_g", [L, D]), ("ln3_b", [L, D])]:
            self.w[name] = self.dram_in(name, shape)
        self.cosT = self.dram_in("cosT", [SEQ + 64, 32])
        self.sinT = self.dram_in("sinT", [SEQ + 64, 32])
        self.cos2 = self.dram_in("cos2", [64, SEQ + 64])
        self.sin2 = self.dram_in("sin2", [64, SEQ + 64])
        self.cmask = self.dram_in("cmask", [128, 3, 128], BF16)
        self.ident_d = self.dram_in("ident", [128, 128], BF16)
        self.y_p = self.dram_out("y_p", [SEQ, D])
        self.y_s = self.dram_out("y_s", [64, D])
        self.o_lat = self.dram_out("o_lat", [L, SEQ, KVL])
        self.o_kr = self.dram_out("o_kr", [L, SEQ, ROPE])
        self.o_sbk = self.dram_out("o_sbk", [L, SEQ, SBW])
        self.o_sbv = self.dram_out("o_sbv", [L, SEQ, SBW])
        self.os_lat = self.dram_out("os_lat", [L, 64, KVL])
        self.os_kr = self.dram_out("os_kr", [L, 64, ROPE])
        self.os_sbk = self.dram_out("os_sbk", [L, 64, SBW])
        self.os_sbv = self.dram_out("os_sbv", [L, 64, SBW])
        self.kv = {}
        for l in range(L):
            for s in range(3):
                K = SEQ if s == 0 else SKEYS
                self.kv[(l, s)] = dict(
                    knT=self.scratch(f"knT{l}{s}", [MH, 128, K]), krT=self.scratch(f"krT{l}{s}", [64, K]),
                    v=self.scratch(f"v{l}{s}", [K, MH * VD]), sbkT=self.scratch(f"sbkT{l}{s}", [SBH, 128, K]),
                    sbv=self.scratch(f"sbv{l}{s}", [K, SBW]))

    def alloc(self):
        nc = self.nc
        self.ps = [self.es.enter_context(nc.psum_tensor(f"ps{i}", [128, 512], F32)) for i in range(8)]
        self.wb = [self.sb(f"wb{i}", [128, WBE], BF16) for i in range(NWB)]
        self.x = self.sb("x", [128, NT, D], F32)
        self.xT = self.sb("xT", [128, 16, TOK], BF16)
        self.a1 = self.sb("a1", [128, 24 * TOK], BF16)
        self.a3 = self.sb("a3", [128, 16 * TOK], BF16)
        self.tmp = self.sb("tmp", [128, 6144], F32)
        self.ident = self.sb("identb", [128, 128], BF16)
        self.masks = self.sb("masks", [128, 3, 128], BF16)
        self.ones = self.sb("ones", [128, 128], BF16)
        self.small = self.sb("small", [128, 64], F32)
        self.gq = self.sb("gq", [128, 2, 512], F32)
        self.ropeT = self.sb("ropeT", [128, NT, 2, 32], F32)
        self.rope2 = self.sb("rope2", [64, 2, TOK], F32)
        self.bcol = self.sb("bcol", [128, 2, 32], F32)
        self.qh = self.sb("qh", [128, 2, 3, 128 * NT], BF16)
        self.wrot = self.sb("wrot", [128, 2, 4, 64], BF16)
        self.pt = self.sb("pt", [128, 3, 512], BF16)
        self.vaug_ones_done = set()

    def consts(self):
        T, nc = self.T, self.nc
        T.dma("sp", self.ident[:], self.ident_d[:, :], writes=[("ident",)])
        T.dma("sp", self.masks[:], self.cmask[:, :, :], writes=[("masks",)])
        T.op("dve", lambda: nc.vector.memset(self.ones[:], 1.0), writes=[("ones",)])

    def bcast_load(self, dst, src_row, n, key):
        self.T.dma("sp", dst, src_row.partition_broadcast(128), writes=[key])

    def transpose_to(self, src, rows, cols, dst, rkeys, wkeys, evac="act"):
        T, nc = self.T, self.nc
        ps, pk = self.psum()
        pv = ps[:].bitcast(BF16)
        T.op("pe", lambda: nc.tensor.transpose(pv[0:cols, 0:rows], src, self.ident[0:rows, 0:rows]),
             reads=list(rkeys) + [("ident",)], writes=[pk])
        if evac == "act":
            T.op("act", lambda: nc.scalar.copy(out=dst, in_=pv[0:cols, 0:rows]), reads=[pk], writes=wkeys)
        else:
            T.op("dve", lambda: nc.vector.tensor_copy(out=dst, in_=pv[0:cols, 0:rows]), reads=[pk], writes=wkeys)

    def transposes4(self, srcs, rows, dst4, rkeys, wkeys, evac="act"):
        T, nc = self.T, self.nc
        n = len(srcs)
        ps, pk = self.psum()
        pv = ps[:].bitcast(BF16).rearrange("p (n r) -> p n r", n=8)
        for i, s in enumerate(srcs):
            T.op("pe", lambda i=i, s=s: nc.tensor.transpose(pv[:, i, 0:rows], s, self.ident[0:rows, 0:rows]),
                 reads=list(rkeys) + [("ident",)], writes=[pk])
        if evac == "act":
            T.op("act", lambda: nc.scalar.copy(out=dst4, in_=pv[:, 0:n, 0:rows]), reads=[pk], writes=wkeys)
        else:
            T.op("dve", lambda: nc.vector.tensor_copy(out=dst4, in_=pv[:, 0:n, 0:rows]), reads=[pk], writes=wkeys)

    def layer_norm(self, t, rows, gname, bname, l, make_xT=True, out_dram=None):
        T, nc = self.T, self.nc
        xt = self.x[0:rows, t, :]
        st = self.small[0:rows, 0:24].rearrange("p (n s) -> p n s", s=6)
        mv = self.small[0:rows, 24:26]
        rs = self.small[0:rows, 26:27]
        nb = self.small[0:rows, 27:28]
        kx = ("x", t)
        for c in range(4):
            T.op("dve", lambda c=c: nc.vector.bn_stats(out=st[:, c, :], in_=xt[:, c * 512:(c + 1) * 512]),
                 reads=[kx], writes=[("lnst",)])
        T.op("dve", lambda: nc.vector.bn_aggr(out=mv, in_=st), reads=[("lnst",)], writes=[("lnmv",)])
        T.op("dve", lambda: nc.vector.tensor_scalar(out=rs, in0=mv[:, 1:2], scalar1=LN_EPS, scalar2=None,
                                                    op0=ALU.add), reads=[("lnmv",)], writes=[("lnrs",)])
        T.op("act", lambda: nc.scalar.activation(out=rs, in_=rs, func=AF.Sqrt), reads=[("lnrs",)], writes=[("lnrs",)])
        T.op("dve", lambda: nc.vector.reciprocal(out=rs, in_=rs), reads=[("lnrs",)], writes=[("lnrs",)])
        T.op("dve", lambda: nc.vector.scalar_tensor_tensor(out=nb, in0=mv[:, 0:1], scalar=-1.0, in1=rs,
                                                           op0=ALU.mult, op1=ALU.mult),
             reads=[("lnmv",), ("lnrs",)], writes=[("lnnb",)])
        T.op("act", lambda: nc.scalar.activation(out=xt, in_=xt, func=AF.Identity, bias=nb, scale=rs),
             reads=[kx, ("lnrs",), ("lnnb",)], writes=[kx])
        gb = self.lnw
        T.op("dve", lambda: nc.vector.tensor_tensor(out=xt, in0=xt, in1=gb[0:rows, 0, :], op=ALU.mult),
             reads=[kx, ("lnw",), self.lnw_key], writes=[kx])
        T.op("dve", lambda: nc.vector.tensor_tensor(out=xt, in0=xt, in1=gb[0:rows, 1, :], op=ALU.add),
             reads=[kx, ("lnw",), self.lnw_key], writes=[kx])
        if out_dram is not None:
            T.dma("sp", out_dram, xt, reads=[kx])
        if make_xT:
            self.make_xT(t, rows)

    def load_lnw(self, gname, bname, l):
        buf, bk = self.wbuf()
        v = buf[:, 0:8192].bitcast(F32).rearrange("p (a d) -> p a d", a=2)
        g = self.w[gname] if l is None else self.w[gname][l]
        b = self.w[bname] if l is None else self.w[bname][l]
        self.T.dma("sp", v[:, 0, :], g.partition_broadcast(128), writes=[bk, ("lnw",)])
        self.T.dma("sp", v[:, 1, :], b.partition_broadcast(128), writes=[bk, ("lnw",)])
        self.lnw = v
        self.lnw_key = bk

    def make_xT(self, t, rows):
        T, nc = self.T, self.nc
        xb = self.tmp[:, 0:1024].bitcast(BF16)
        T.op("act", lambda: nc.scalar.copy(out=xb[0:rows, :], in_=self.x[0:rows, t, :]),
             reads=[("x", t)], writes=[("xb",), ("R0",), ("R1",)])
        for g in range(2):
            srcs = [xb[0:rows, (g * 8 + i) * 128:(g * 8 + i + 1) * 128] for i in range(8)]
            self.transposes4(srcs, rows, self.xT[:, g * 8:(g + 1) * 8, t * 128:t * 128 + rows],
                             [("xb",)], [("xT", t)], evac="dve" if g else "act")

    def load_w(self, src, kchunks, cols, queue="pool"):
        buf, bk = self.wbuf()
        v = buf[:, 0:kchunks * cols].rearrange("p (k c) -> p k c", k=kchunks)
        self.T.dma(queue, v, src.rearrange("(k p) c -> p k c", p=128), writes=[bk], nobar=True)
        return v, bk

    def stage_A(self, l, rows_list, src_list, pos0, sample):
        T, nc = self.T, self.nc
        ntl = len(rows_list)
        ntok = TOK if not sample else 64
        w_in = self.w["w_in"][l]
        self.q_aT = self.a3[:, 0:4 * TOK].rearrange("p (c t) -> p c t", c=4)
        self.sbqT = self.a3[:, 4 * TOK:12 * TOK].rearrange("p (h t) -> p h t", h=8)
        stg = self.a1
        lat_bf = stg[:, 0:NT * 512].rearrange("p (t c) -> p t c", t=NT)
        kr_bf = stg[:, NT * 512:NT * 576].rearrange("p (t c) -> p t c", t=NT)
        sbk_bf = stg[:, NT * 576:NT * 1600].rearrange("p (t c) -> p t c", t=NT)
        sbv_bf = stg[:, NT * 1600:NT * 2624].rearrange("p (t c) -> p t c", t=NT)
        f32t = self.tmp[:, 1024:1536]
        f32u = self.tmp[:, 1536:2048]
        sq = self.tmp[:, 2048:2560]
        bft = self.tmp[:, 2560:2816].bitcast(BF16)
        if sample:
            o_lat, o_kr, o_sbk, o_sbv = self.os_lat[l], self.os_kr[l], self.os_sbk[l], self.os_sbv[l]
            r0 = 0
        else:
            o_lat, o_kr, o_sbk, o_sbv = self.o_lat[l], self.o_kr[l], self.o_sbk[l], self.o_sbv[l]
            r0 = pos0
        self.bcast_load(self.gq[:, 0, :], self.w["q_a_norm_g"][l], 512, ("gq",))
        self.bcast_load(self.gq[:, 1, :], self.w["kv_a_norm_g"][l], 512, ("gq",))
        cgs = [(0, 512, "qa"), (512, 512, "kva"), (1024, 64, "kr"), (1088, 512, "sbq0"), (1600, 512, "sbq1"),
               (2112, 512, "sbk0"), (2624, 512, "sbk1"), (3136, 512, "sbv0"), (3648, 512, "sbv1")]
        fl = [0]

        def stage_buf():
            fl[0] ^= 1
            return (f32t, ("f32t",)) if fl[0] else (f32u, ("f32u",))

        import os
        cgs = cgs[:int(os.environ.get("KCG", "9"))]
        for (c0, cw, kind) in cgs:
            wv, wk = self.load_w(w_in[:, c0:c0 + cw], 16, cw)
            for t in range(ntl):
                rows = rows_list[t]
                ps, pk = self.psum()
                for k in range(16):
                    T.op("pe", lambda k=k, t=t, rows=rows, ps=ps, wv=wv, cw=cw: nc.tensor.matmul(
                        ps[0:rows, 0:cw], self.xT[:, k, t * 128:t * 128 + rows], wv[:, k, :],
                        start=(k == 0), stop=(k == 15)), reads=[("xT", t), wk], writes=[pk])
                orow = slice(r0 + t * 128, r0 + t * 128 + rows)
                if kind in ("qa", "kva"):
                    gi = 0 if kind == "qa" else 1
                    ss = self.small[0:rows, 32 + gi:33 + gi]
                    T.op("dve", lambda ss=ss: nc.vector.memset(ss, 0.0), writes=[("ss", gi)])
                    T.op("act", lambda ps=ps, rows=rows, ss=ss: nc.scalar.activation(
                        out=sq[0:rows, :], in_=ps[0:rows, :], func=AF.Square, accum_out=ss),
                        reads=[pk, ("ss", gi)], writes=[("sq",), ("ss", gi)])
                    T.op("dve", lambda ss=ss: nc.vector.tensor_scalar(out=ss, in0=ss, scalar1=1.0 / 512, scalar2=RMS_EPS,
                                                                      op0=ALU.mult, op1=ALU.add),
                         reads=[("ss", gi)], writes=[("ss", gi)])
                    T.op("act", lambda ss=ss: nc.scalar.activation(out=ss, in_=ss, func=AF.Sqrt),
                         reads=[("ss", gi)], writes=[("ss", gi)])
                    T.op("dve", lambda ss=ss: nc.vector.reciprocal(out=ss, in_=ss), reads=[("ss", gi)], writes=[("ss", gi)])
                    if kind == "qa":
                        T.op("dve", lambda ps=ps, rows=rows, ss=ss: nc.vector.scalar_tensor_tensor(
                            out=bft[0:rows, :], in0=ps[0:rows, :], scalar=ss, in1=self.gq[0:rows, 0, :],
                            op0=ALU.mult, op1=ALU.mult), reads=[pk, ("ss", gi), ("gq",)], writes=[("bft",)])
                        srcs = [bft[0:rows, i * 128:(i + 1) * 128] for i in range(4)]
                        self.transposes4(srcs, rows, self.q_aT[:, :, t * 128:t * 128 + rows], [("bft",)], [("q_aT", t)])
                    else:
                        fb, fk = stage_buf()
                        T.op("dve", lambda ps=ps, rows=rows, ss=ss, fb=fb: nc.vector.scalar_tensor_tensor(
                            out=fb[0:rows, :], in0=ps[0:rows, :], scalar=ss, in1=self.gq[0:rows, 1, :],
                            op0=ALU.mult, op1=ALU.mult), reads=[pk, ("ss", gi), ("gq",)], writes=[fk])
                        T.dma("sp", o_lat[orow, :], fb[0:rows, :], reads=[fk])
                        T.op("act", lambda rows=rows, t=t, fb=fb: nc.scalar.copy(out=lat_bf[0:rows, t, :], in_=fb[0:rows, :]),
                             reads=[fk], writes=[("kstg", t)])
                elif kind == "kr":
                    fb, fk = stage_buf()
                    x1 = ps[0:rows, 0:32]; x2 = ps[0:rows, 32:64]
                    cs = self.ropeT[0:rows, t, 0, :]; sn = self.ropeT[0:rows, t, 1, :]
                    ta = sq[0:rows, 0:32]; tb = sq[0:rows, 32:64]
                    T.op("dve", lambda: nc.vector.tensor_tensor(out=ta, in0=x1, in1=cs, op=ALU.mult),
                         reads=[pk, ("ropeT",)], writes=[("sq",)])
                    T.op("dve", lambda: nc.vector.tensor_tensor(out=tb, in0=x2, in1=sn, op=ALU.mult),
                         reads=[pk, ("ropeT",)], writes=[("sq",)])
                    T.op("dve", lambda: nc.vector.tensor_tensor(out=fb[0:rows, 0:32], in0=ta, in1=tb, op=ALU.subtract),
                         reads=[("sq",)], writes=[fk])
                    T.op("dve", lambda: nc.vector.tensor_tensor(out=ta, in0=x2, in1=cs, op=ALU.mult),
                         reads=[pk, ("ropeT",), fk], writes=[("sq",)])
                    T.op("dve", lambda: nc.vector.tensor_tensor(out=tb, in0=x1, in1=sn, op=ALU.mult),
                         reads=[pk, ("ropeT",)], writes=[("sq",)])
                    T.op("dve", lambda: nc.vector.tensor_tensor(out=fb[0:rows, 32:64], in0=ta, in1=tb, op=ALU.add),
                         reads=[("sq",)], writes=[fk])
                    T.dma("sp", o_kr[orow, :], fb[0:rows, 0:64], reads=[fk])
                    T.op("act", lambda rows=rows, t=t, fb=fb: nc.scalar.copy(out=kr_bf[0:rows, t, :], in_=fb[0:rows, 0:64]),
                         reads=[fk], writes=[("kstg", t)])
                elif kind.startswith("sbq"):
                    hh = int(kind[-1])
                    T.op("act", lambda ps=ps, rows=rows: nc.scalar.copy(out=bft[0:rows, :], in_=ps[0:rows, :]),
                         reads=[pk], writes=[("bft",)])
                    srcs = [bft[0:rows, i * 128:(i + 1) * 128] for i in range(4)]
                    self.transposes4(srcs, rows, self.sbqT[:, hh * 4:hh * 4 + 4, t * 128:t * 128 + rows],
                                     [("bft",)], [("sbqT", t)], evac="dve")
                else:
                    hh = int(kind[-1])
                    isk = kind.startswith("sbk")
                    fb, fk = stage_buf()
                    dst_bf = (sbk_bf if isk else sbv_bf)
                    odr = (o_sbk if isk else o_sbv)
                    T.op("act", lambda ps=ps, rows=rows, fb=fb: nc.scalar.copy(out=fb[0:rows, :], in_=ps[0:rows, :]),
                         reads=[pk], writes=[fk])
                    T.dma("sp", odr[orow, hh * 512:(hh + 1) * 512], fb[0:rows, :], reads=[fk])
                    T.op("dve", lambda fb=fb, rows=rows, t=t, dst_bf=dst_bf, hh=hh: nc.vector.tensor_copy(
                        out=dst_bf[0:rows, t, hh * 512:(hh + 1) * 512], in_=fb[0:rows, :]),
                        reads=[fk], writes=[("kstg", t)])
        import os
        if os.environ.get("KNOKPREP"):
            return
        self.load_wkv(l)
        if not sample:
            for t in range(ntl):
                self.kprep(l, 0, pos0 + t * 128, 128, lat_bf[:, t, :], kr_bf[:, t, :], sbk_bf[:, t, :], sbv_bf[:, t, :],
                           [("kstg", t)])
        else:
            self.kprep(l, 1, PAST, 32, lat_bf[:, 0, :], kr_bf[:, 0, :], sbk_bf[:, 0, :], sbv_bf[:, 0, :], [("kstg", 0)])
            self.kprep(l, 2, PAST, 32, lat_bf[:, 0, :], kr_bf[:, 0, :], sbk_bf[:, 0, :], sbv_bf[:, 0, :], [("kstg", 0)],
                       prow=32)
        self.release(self.wuk_k)
        self.release(self.wuv_k)

    def load_wkv(self, l):
        wkv = self.w["w_kv_b"][l].rearrange("(k p) (h t d) -> p k h t d", p=128, h=MH, t=2)
        b0, k0 = self.wbuf(hold=True)
        b1, k1 = self.wbuf(hold=True)
        self.wuk = b0[:, 0:4 * 2048].rearrange("p (k h d) -> p k h d", k=4, h=MH)
        self.wuv = b1[:, 0:4 * 2048].rearrange("p (k h d) -> p k h d", k=4, h=MH)
        for k in range(4):
            self.T.dma("pool", self.wuk[:, k, :, :], wkv[:, k, :, 0, :], writes=[k0], nobar=True)
            self.T.dma("pool", self.wuv[:, k, :, :], wkv[:, k, :, 1, :], writes=[k1], nobar=True)
        self.wuk_k, self.wuv_k = k0, k1

    def kprep(self, l, s, key0, nk, lat_bf, kr_bf, sbk_bf, sbv_bf, rkeys, prow=0):
        T, nc = self.T, self.nc
        kv = self.kv[(l, s)]
        kk = ("kv", l, s)
        pr = slice(prow, prow + nk)
        wk_ = list(rkeys)
        if prow != 0:
            bnc = self.bounce
            T.dma("sp", bnc[0:nk, 0:512], lat_bf[pr, :], reads=wk_, writes=[("bnc",)])
            T.dma("sp", bnc[0:nk, 512:576], kr_bf[pr, :], reads=wk_, writes=[("bnc",)])
            T.dma("sp", bnc[0:nk, 576:1600], sbk_bf[pr, :], reads=wk_, writes=[("bnc",)])
            T.dma("sp", bnc[0:nk, 1600:2624], sbv_bf[pr, :], reads=wk_, writes=[("bnc",)])
            st2 = self.tmp[:, 1024:2336].bitcast(BF16)
            T.dma("sp", st2[0:nk, :], bnc[0:nk, :], reads=[("bnc",)], writes=[("st2",), ("f32t",), ("f32u",), ("sq",)])
            lat_bf, kr_bf, sbk_bf, sbv_bf = st2[:, 0:512], st2[:, 512:576], st2[:, 576:1600], st2[:, 1600:2624]
            wk_ = [("st2",)]
            pr = slice(0, nk)
        ksl = slice(key0, key0 + nk)
        T.dma("sp", kv["sbv"][ksl, :], sbv_bf[pr, :], reads=wk_, writes=[kk])
        latT = self.tmp[:, 2816:3072].bitcast(BF16).rearrange("p (c k) -> p c k", c=4)
        self.transposes4([lat_bf[pr, c * 128:(c + 1) * 128] for c in range(4)], nk, latT[:, :, 0:nk], wk_, [("latT",)])
        krT = self.tmp[:, 3072:3136].bitcast(BF16)
        self.transpose_to(kr_bf[pr, :], nk, 64, krT[0:64, 0:nk], wk_, [("krTs",)], evac="dve")
        T.dma("sp", kv["krT"][:, ksl], krT[0:64, 0:nk], reads=[("krTs",)], writes=[kk])
        sbkT = self.tmp[:, 3136:3648].bitcast(BF16).rearrange("p (h k) -> p h k", h=8)
        self.transposes4([sbk_bf[pr, h * 128:(h + 1) * 128] for h in range(8)], nk, sbkT[:, :, 0:nk], wk_, [("sbkTs",)],
                         evac="dve")
        T.dma("sp", kv["sbkT"][:, :, ksl].rearrange("h p k -> p h k"), sbkT[:, :, 0:nk], reads=[("sbkTs",)], writes=[kk])
        knTs = self.tmp[:, 3648:4672].bitcast(BF16).rearrange("p (h k) -> p h k", h=16)
        for hg in range(4):
            ps, pk = self.psum()
            pv = ps[:].rearrange("p (h k) -> p h k", h=4)
            for hh in range(4):
                h = hg * 4 + hh
                for c in range(4):
                    T.op("pe", lambda h=h, hh=hh, c=c, pv=pv: nc.tensor.matmul(
                        pv[:, hh, 0:nk], self.wuk[:, c, h, :], latT[:, c, 0:nk], start=(c == 0), stop=(c == 3)),
                        reads=[("latT",), self.wuk_k], writes=[pk])
            T.op("act" if hg % 2 == 0 else "dve",
                 (lambda pv=pv, hg=hg: nc.scalar.copy(out=knTs[:, hg * 4:hg * 4 + 4, 0:nk], in_=pv[:, :, 0:nk])) if hg % 2 == 0
                 else (lambda pv=pv, hg=hg: nc.vector.tensor_copy(out=knTs[:, hg * 4:hg * 4 + 4, 0:nk], in_=pv[:, :, 0:nk])),
                 reads=[pk], writes=[("knTs", hg)])
        T.dma("sp", kv["knT"][:, :, ksl].rearrange("h p k -> p h k"), knTs[:, :, 0:nk],
              reads=[("knTs", g) for g in range(4)], writes=[kk])
        vs = self.tmp[:, 4672:5696].bitcast(BF16)
        for cg in range(4):
            ps, pk = self.psum()
            for c in range(4):
                T.op("pe", lambda c=c, cg=cg, ps=ps: nc.tensor.matmul(
                    ps[0:nk, :], latT[:, c, 0:nk], self.wuv[:, c, cg * 4:cg * 4 + 4, :].rearrange("p h d -> p (h d)"),
                    start=(c == 0), stop=(c == 3)), reads=[("latT",), self.wuv_k], writes=[pk])
            T.op("act" if cg % 2 else "dve",
                 (lambda ps=ps, cg=cg: nc.scalar.copy(out=vs[0:nk, cg * 512:(cg + 1) * 512], in_=ps[0:nk, :])) if cg % 2
                 else (lambda ps=ps, cg=cg: nc.vector.tensor_copy(out=vs[0:nk, cg * 512:(cg + 1) * 512], in_=ps[0:nk, :])),
                 reads=[pk], writes=[("vs", cg)])
        T.dma("sp", kv["v"][ksl, :], vs[0:nk, :], reads=[("vs", g) for g in range(4)], writes=[kk])

    def cache_prep(self, l):
        T = self.T
        st2 = self.a1[:, 0:2 * 2624].rearrange("p (b c) -> p b c", b=2)
        self.load_wkv(l)
        i = 0
        for s in (1, 2):
            for kb in range(PAST // 128):
                b = i % 2
                i += 1
                rs = slice(kb * 128, (kb + 1) * 128)
                key = ("cst", b)
                T.dma("pool", st2[:, b, 0:512], self.c_lat[l, s - 1, rs, :], writes=[key])
                T.dma("pool", st2[:, b, 512:576], self.c_kr[l, s - 1, rs, :], writes=[key])
                T.dma("pool", st2[:, b, 576:1600], self.c_sbk[l, s - 1, rs, :], writes=[key])
                T.dma("pool", st2[:, b, 1600:2624], self.c_sbv[l, s - 1, rs, :], writes=[key])
                self.kprep(l, s, kb * 128, 128, st2[:, b, 0:512], st2[:, b, 512:576], st2[:, b, 576:1600],
                           st2[:, b, 1600:2624], [key])
        self.release(self.wuk_k)
        self.release(self.wuv_k)

    def stage_B(self, l, segs, pos0, sample):
        T, nc = self.T, self.nc
        self.oT = self.a1[:, 0:24 * TOK].rearrange("p (c t) -> p c t", c=24)
        sources = sorted(set(s[0] for s in segs))
        for h in range(SBH):
            for src in sources:
                ssegs = [s for s in segs if s[0] == src]
                nblk = max(b[0] for s in ssegs for b in s[3]) + 1
                kv = self.kv[(l, src)]
                buf, bk = self.wbuf()
                kT = buf[:, 0:4096]
                vv = buf[:, 4096:4096 + 32 * 128].rearrange("p (b d) -> p b d", b=32)
                nkeys = sum(b[1] for b in max(ssegs, key=lambda s: len(s[3]))[3])
                T.dma("sp", kT[:, 0:nkeys], kv["sbkT"][h, :, 0:nkeys], reads=[("kv", l, src)], writes=[bk])
                nfull = nkeys // 128
                if nfull:
                    T.dma("sp", vv[:, 0:nfull, :], kv["sbv"][0:nfull * 128, h * 128:(h + 1) * 128].rearrange(
                        "(b p) d -> p b d", p=128), reads=[("kv", l, src)], writes=[bk])
                if nkeys % 128:
                    r = nkeys % 128
                    T.dma("sp", vv[0:r, nfull, :], kv["sbv"][nfull * 128:nkeys, h * 128:(h + 1) * 128],
                          reads=[("kv", l, src)], writes=[bk])
                for (_, c0, nq, blocks) in ssegs:
                    self.sb_segment(h, c0, nq, blocks, kT, vv, bk)
        krbuf, krk = self.wbuf(hold=True)
        wqk = None
        wq = self.w["w_q_b"][l]
        for src in sources:
            ssegs = [s for s in segs if s[0] == src]
            kv = self.kv[(l, src)]
            nkeys = sum(b[1] for b in max(ssegs, key=lambda s: len(s[3]))[3])
            krT = krbuf[0:64, 0:4096]
            T.dma("sp", krT[:, 0:nkeys], kv["krT"][:, 0:nkeys], reads=[("kv", l, src)], writes=[krk])
            wqv = None
            for h in range(MH):
                if h % 8 == 0:
                    if wqk is not None:
                        self.release(wqk)
                    wbuf_, wqk = self.wbuf(hold=True)
                    wqv = wbuf_[:, 0:4 * 1536].rearrange("p (k c) -> p k c", k=4)
                    T.dma("pool", wqv, wq[:, (h // 8) * 1536:(h // 8 + 1) * 1536].rearrange("(k p) c -> p k c", p=128),
                          writes=[wqk], nobar=True)
                buf, bk = self.wbuf()
                kT = buf[:, 0:4096]
                va = buf[:, 4096:4096 + 32 * 129].rearrange("p (b d) -> p b d", b=32)
                T.dma("sp", kT[:, 0:nkeys], kv["knT"][h, :, 0:nkeys], reads=[("kv", l, src)], writes=[bk])
                nfull = nkeys // 128
                if nfull:
                    T.dma("sp", va[:, 0:nfull, 0:128], kv["v"][0:nfull * 128, h * 128:(h + 1) * 128].rearrange(
                        "(b p) d -> p b d", p=128), reads=[("kv", l, src)], writes=[bk])
                if nkeys % 128:
                    r = nkeys % 128
                    T.dma("sp", va[0:r, nfull, 0:128], kv["v"][nfull * 128:nkeys, h * 128:(h + 1) * 128],
                          reads=[("kv", l, src)], writes=[bk])
                T.op("dve", lambda va=va: nc.vector.memset(va[:, :, 128:129], 1.0), writes=[bk])
                cmin = min(s[1] for s in ssegs)
                cmax = max(s[1] + s[2] for s in ssegs)
                qb = h % 2
                self.make_q(h, wqv, wqk, qb, cmin, cmax)
                for (_, c0, nq, blocks) in ssegs:
                    self.mla_segment(h, qb, c0, nq, blocks, kT, krT, va, bk, krk)
            self.release(wqk)
            wqk = None
        self.release(krk)

    def make_q(self, h, wqv, wqk, qb, c0, c1):
        T, nc = self.T, self.nc
        n = c1 - c0
        hb = (h % 8) * 192
        qn = self.qh[:, qb, 0, :]
        qr = self.qh[0:64, qb, 1, :]
        wr = self.wrot[:, qb, :, :]
        kq = ("qh", qb)
        T.op("dve", lambda: nc.vector.tensor_scalar(out=wr[:, :, 0:32], in0=wqv[:, :, hb + 160:hb + 192], scalar1=-1.0,
                                                    scalar2=None, op0=ALU.mult), reads=[wqk], writes=[("wrot", qb)])
        T.op("dve", lambda: nc.vector.tensor_copy(out=wr[:, :, 32:64], in_=wqv[:, :, hb + 128:hb + 160]),
             reads=[wqk], writes=[("wrot", qb)])
        ps, pk = self.psum()
        for c in range(4):
            T.op("pe", lambda c=c: nc.tensor.matmul(ps[:, 0:n], wqv[:, c, hb:hb + 128], self.q_aT[:, c, c0:c1],
                                                    start=(c == 0), stop=(c == 3)),
                 reads=[wqk, ("q_aT", "all")], writes=[pk])
        T.op("act", lambda: nc.scalar.copy(out=qn[:, c0:c1], in_=ps[:, 0:n]), reads=[pk], writes=[kq])
        ps1, pk1 = self.psum()
        ps2, pk2 = self.psum()
        for c in range(4):
            T.op("pe", lambda c=c: nc.tensor.matmul(ps1[0:64, 0:n], wqv[:, c, hb + 128:hb + 192], self.q_aT[:, c, c0:c1],
                                                    start=(c == 0), stop=(c == 3)),
                 reads=[wqk, ("q_aT", "all")], writes=[pk1])
        for c in range(4):
            T.op("pe", lambda c=c: nc.tensor.matmul(ps2[0:64, 0:n], wr[:, c, :], self.q_aT[:, c, c0:c1],
                                                    start=(c == 0), stop=(c == 3)),
                 reads=[("wrot", qb), ("q_aT", "all")], writes=[pk2])
        t1 = self.tmp[0:64, 0:512]
        t2 = self.tmp[0:64, 512:1024]
        T.op("dve", lambda: nc.vector.tensor_tensor(out=t1[:, 0:n], in0=ps1[0:64, 0:n], in1=self.rope2[:, 0, c0:c1],
                                                    op=ALU.mult), reads=[pk1, ("rope2",)], writes=[("R0",)])
        T.op("dve", lambda: nc.vector.tensor_tensor(out=t2[:, 0:n], in0=ps2[0:64, 0:n], in1=self.rope2[:, 1, c0:c1],
                                                    op=ALU.mult), reads=[pk2, ("rope2",)], writes=[("R1",)])
        T.op("dve", lambda: nc.vector.tensor_tensor(out=qr[:, c0:c1], in0=t1[:, 0:n], in1=t2[:, 0:n], op=ALU.add),
             reads=[("R0",), ("R1",)], writes=[kq])

    def mla_segment(self, h, qb, c0, nq, blocks, kT, krT, va, bk, krk):
        T, nc = self.T, self.nc
        qn = self.qh[:, qb, 0, c0:c0 + nq]
        qr = self.qh[0:64, qb, 1, c0:c0 + nq]
        kq = ("qh", qb)
        o_ps, o_k = self.psum(hold=True)
        nb = len(blocks)
        gi = 0
        key_off = 0
        offs = []
        for (blk, nk, m) in blocks:
            offs.append(key_off)
            key_off += nk
        for g0 in range(0, nb, 4):
            grp = blocks[g0:g0 + 4]
            s_ps, s_k = self.psum()
            sv = s_ps[:].rearrange("p (b q) -> p b q", b=4)
            for i, (blk, nk, m) in enumerate(grp):
                ko = offs[g0 + i]
                T.op("pe", lambda i=i, nk=nk, ko=ko: nc.tensor.matmul(sv[0:nk, i, 0:nq], kT[:, ko:ko + nk], qn,
                                                                       start=True, stop=False),
                     reads=[bk, kq], writes=[s_k])
                T.op("pe", lambda i=i, nk=nk, ko=ko: nc.tensor.matmul(sv[0:nk, i, 0:nq], krT[:, ko:ko + nk], qr,
                                                                       start=False, stop=True),
                     reads=[krk, kq], writes=[s_k])
            pi = gi % 3
            gi += 1
            pt = self.pt[:, pi, :].rearrange("p (b q) -> p b q", b=4)
            pkey = ("pt", pi)
            full = [i for i, b in enumerate(grp) if b[1] == 128]
            part = [i for i, b in enumerate(grp) if b[1] != 128]
            if full:
                nf = len(full)
                T.op("act", lambda nf=nf: nc.scalar.activation(out=pt[:, 0:nf, 0:nq], in_=sv[:, 0:nf, 0:nq], func=AF.Exp,
                                                               scale=MLA_SCALE), reads=[s_k], writes=[pkey])
            for i in part:
                nk = grp[i][1]
                T.op("act", lambda i=i, nk=nk: nc.scalar.activation(out=pt[0:nk, i, 0:nq], in_=sv[0:nk, i, 0:nq],
                                                                    func=AF.Exp, scale=MLA_SCALE),
                     reads=[s_k], writes=[pkey])
            for i, (blk, nk, m) in enumerate(grp):
                if m is not None:
                    T.op("dve", lambda i=i, nk=nk, m=m: nc.vector.tensor_tensor(
                        out=pt[0:nk, i, 0:nq], in0=pt[0:nk, i, 0:nq], in1=self.masks[0:nk, m, 0:nq], op=ALU.mult),
                        reads=[pkey, ("masks",)], writes=[pkey])
            for i, (blk, nk, m) in enumerate(grp):
                bi = g0 + i
                T.op("pe", lambda i=i, nk=nk, bi=bi, blk=blk: nc.tensor.matmul(
                    o_ps[0:nq, 0:129], pt[0:nk, i, 0:nq], va[0:nk, blk, :], start=(bi == 0), stop=(bi == nb - 1)),
                    reads=[pkey, bk], writes=[o_k])
        rc = self.small[0:nq, 40:41]
        T.op("dve", lambda: nc.vector.reciprocal(out=rc, in_=o_ps[0:nq, 128:129]), reads=[o_k], writes=[("rc",)])
        ob = self.tmp[:, 5696:5760].bitcast(BF16)
        T.op("dve", lambda: nc.vector.tensor_scalar(out=ob[0:nq, :], in0=o_ps[0:nq, 0:128], scalar1=rc, scalar2=None,
                                                    op0=ALU.mult), reads=[o_k, ("rc",)], writes=[("ob",)])
        self.release(o_k)
        self.transpose_to(ob[0:nq, :], nq, 128, self.oT[:, h, c0:c0 + nq], [("ob",)], [("oT", h)])

    def sb_segment(self, h, c0, nq, blocks, kT, vv, bk):
        T, nc = self.T, self.nc
        q = self.sbqT[:, h, c0:c0 + nq]
        o_ps, o_k = self.psum(hold=True)
        nb = len(blocks)
        offs = []
        ko = 0
        for (blk, nk, m) in blocks:
            offs.append(ko)
            ko += nk
        cs = self.tmp[:, 5760:5888]
        e_t = self.tmp[:, 5888:6016]
        l_t = self.tmp[:, 6016:6144]
        r_bf = self.tmp[:, 2816:2880].bitcast(BF16)
        first = True
        for bi in range(nb - 1, -1, -1):
            blk, nk, m = blocks[bi]
            ko = offs[bi]
            z_ps, z_k = self.psum()
            T.op("pe", lambda nk=nk, ko=ko: nc.tensor.matmul(z_ps[0:nk, 0:nq], kT[:, ko:ko + nk], q, start=True, stop=True),
                 reads=[bk, ("sbqT", "all")], writes=[z_k])
            T.op("act", lambda nk=nk: nc.scalar.activation(out=e_t[0:nk, 0:nq], in_=z_ps[0:nk, 0:nq], func=AF.Exp,
                                                           scale=-SB_SCALE), reads=[z_k], writes=[("sbe",)])
            T.op("act", lambda nk=nk: nc.scalar.activation(out=l_t[0:nk, 0:nq], in_=e_t[0:nk, 0:nq], func=AF.Ln, bias=1.0),
                 reads=[("sbe",)], writes=[("sbl",)])
            T.op("dve", lambda nk=nk: nc.vector.scalar_tensor_tensor(out=r_bf[0:nk, 0:nq], in0=z_ps[0:nk, 0:nq],
                                                                     scalar=SB_SCALE, in1=l_t[0:nk, 0:nq],
                                                                     op0=ALU.mult, op1=ALU.add),
                 reads=[z_k, ("sbl",)], writes=[("sbr",)])
            if m is not None:
                T.op("dve", lambda nk=nk, m=m: nc.vector.tensor_tensor(out=r_bf[0:nk, 0:nq], in0=r_bf[0:nk, 0:nq],
                                                                       in1=self.masks[0:nk, m, 0:nq], op=ALU.mult),
                     reads=[("sbr",), ("masks",)], writes=[("sbr",)])
            t_ps, t_k = self.psum()
            T.op("pe", lambda nk=nk: nc.tensor.matmul(t_ps[0:nk, 0:nq], self.masks[0:nk, 2, 0:nk], r_bf[0:nk, 0:nq],
                                                      start=True, stop=True), reads=[("sbr",), ("masks",)], writes=[t_k])
            T.op("pe", lambda nk=nk: nc.tensor.matmul(t_ps[:, 128:128 + nq], self.ones[0:nk, :], r_bf[0:nk, 0:nq],
                                                      start=True, stop=True), reads=[("sbr",), ("ones",)], writes=[t_k])
            T.op("dve", lambda nk=nk: nc.vector.tensor_tensor(out=e_t[0:nk, 0:nq], in0=t_ps[0:nk, 0:nq], in1=l_t[0:nk, 0:nq],
                                                              op=ALU.add), reads=[t_k, ("sbl",), ("sbe",)], writes=[("sbe",)])
            if not first:
                T.op("dve", lambda nk=nk: nc.vector.tensor_tensor(out=e_t[0:nk, 0:nq], in0=e_t[0:nk, 0:nq],
                                                                  in1=cs[0:nk, 0:nq], op=ALU.add),
                     reads=[("sbe",), ("sbcs",)], writes=[("sbe",)])
            a_bf = self.pt[:, bi % 3, 0:128]
            akey = ("pt", bi % 3)
            T.op("act", lambda nk=nk, a_bf=a_bf: nc.scalar.activation(out=a_bf[0:nk, 0:nq], in_=e_t[0:nk, 0:nq], func=AF.Exp,
                                                                      scale=-1.0), reads=[("sbe",)], writes=[akey])
            if m is not None:
                T.op("dve", lambda nk=nk, m=m, a_bf=a_bf: nc.vector.tensor_tensor(
                    out=a_bf[0:nk, 0:nq], in0=a_bf[0:nk, 0:nq], in1=self.masks[0:nk, m, 0:nq], op=ALU.mult),
                    reads=[akey, ("masks",)], writes=[akey])
            if bi > 0:
                if first:
                    T.op("dve", lambda: nc.vector.tensor_copy(out=cs[:, 0:nq], in_=t_ps[:, 128:128 + nq]),
                         reads=[t_k], writes=[("sbcs",)])
                else:
                    T.op("dve", lambda: nc.vector.tensor_tensor(out=cs[:, 0:nq], in0=cs[:, 0:nq], in1=t_ps[:, 128:128 + nq],
                                                                op=ALU.add), reads=[t_k, ("sbcs",)], writes=[("sbcs",)])
            T.op("pe", lambda nk=nk, blk=blk, a_bf=a_bf, bi=bi: nc.tensor.matmul(
                o_ps[0:nq, 0:128], a_bf[0:nk, 0:nq], vv[0:nk, blk, :], start=(bi == nb - 1), stop=(bi == 0)),
                reads=[akey, bk], writes=[o_k])
            first = False
        ob = self.tmp[:, 5696:5760].bitcast(BF16)
        T.op("act", lambda: nc.scalar.copy(out=ob[0:nq, :], in_=o_ps[0:nq, 0:128]), reads=[o_k], writes=[("ob",)])
        self.release(o_k)
        self.transpose_to(ob[0:nq, :], nq, 128, self.oT[:, 16 + h, c0:c0 + nq], [("ob",)], [("oT", 16 + h)], evac="dve")

    def stage_C(self, l, rows_list):
        T, nc = self.T, self.nc
        ntl = len(rows_list)
        ntok = sum(rows_list)
        self.mT = self.a3[:, 0:16 * TOK].rearrange("p (c t) -> p c t", c=16)
        wa, wbb, win = self.w["w_branch_a"][l], self.w["w_branch_b"][l], self.w["w_in"][l]
        for a in range(2):
            for c in range(16):
                T.dma("sp", self.bcol[:, a, c:c + 1],
                      self.w["b_gate"][l][(a * 16 + c) * 128:(a * 16 + c + 1) * 128].rearrange("(p o) -> p o", o=1),
                      writes=[("bcol",)])
        ga = self.tmp[:, 0:512]
        gb = self.tmp[:, 512:1024]
        t1 = self.tmp[:, 1024:1536]
        for c in range(16):
            cc = c % 4
            if cc == 0:
                g0 = (c // 4) * 512
                bA, kA = self.wbuf()
                bB, kB = self.wbuf()
                bC, kC = self.wbuf()
                bD, kD = self.wbuf()
                wA = bA[:, 0:16 * 512].rearrange("p (k c) -> p k c", k=16)
                wB = bB[:, 0:8 * 512].rearrange("p (k c) -> p k c", k=8)
                wGa = bC[:, 0:16 * 512].rearrange("p (k c) -> p k c", k=16)
                wGb = bD[:, 0:16 * 512].rearrange("p (k c) -> p k c", k=16)
                T.dma("pool", wA, wa[:, g0:g0 + 512].rearrange("(k p) c -> p k c", p=128), writes=[kA], nobar=True)
                T.dma("pool", wB, wbb[:, g0:g0 + 512].rearrange("(k p) c -> p k c", p=128), writes=[kB], nobar=True)
                T.dma("pool", wGa, win[:, 4160 + g0:4160 + g0 + 512].rearrange("(k p) c -> p k c", p=128), writes=[kC],
                      nobar=True)
                T.dma("pool", wGb, win[:, 6208 + g0:6208 + g0 + 512].rearrange("(k p) c -> p k c", p=128), writes=[kD],
                      nobar=True)
            csl = slice(cc * 128, (cc + 1) * 128)
            pa, ka = self.psum()
            pb, kb = self.psum()
            pga, kga = self.psum()
            pgb, kgb = self.psum()
            for k in range(16):
                T.op("pe", lambda k=k: nc.tensor.matmul(pa[:, 0:ntok], wA[:, k, csl], self.oT[:, k, 0:ntok],
                                                        start=(k == 0), stop=(k == 15)), reads=[kA, ("oT", k)], writes=[ka])
            for k in range(8):
                T.op("pe", lambda k=k: nc.tensor.matmul(pb[:, 0:ntok], wB[:, k, csl], self.oT[:, 16 + k, 0:ntok],
                                                        start=(k == 0), stop=(k == 7)), reads=[kB, ("oT", 16 + k)], writes=[kb])
            for k in range(16):
                T.op("pe", lambda k=k: nc.tensor.matmul(pga[:, 0:ntok], wGa[:, k, csl], self.xT[:, k, 0:ntok],
                                                        start=(k == 0), stop=(k == 15)), reads=[kC, ("xT", "all")], writes=[kga])
            for k in range(16):
                T.op("pe", lambda k=k: nc.tensor.matmul(pgb[:, 0:ntok], wGb[:, k, csl], self.xT[:, k, 0:ntok],
                                                        start=(k == 0), stop=(k == 15)), reads=[kD, ("xT", "all")], writes=[kgb])
            T.op("act", lambda c=c: nc.scalar.activation(out=ga[:, 0:ntok], in_=pga[:, 0:ntok], func=AF.Sigmoid,
                                                         bias=self.bcol[:, 0, c:c + 1]), reads=[kga, ("bcol",)], writes=[("R0",)])
            T.op("act", lambda c=c: nc.scalar.activation(out=gb[:, 0:ntok], in_=pgb[:, 0:ntok], func=AF.Sigmoid,
                                                         bias=self.bcol[:, 1, c:c + 1]), reads=[kgb, ("bcol",)], writes=[("R1",)])
            T.op("dve", lambda: nc.vector.tensor_tensor(out=ga[:, 0:ntok], in0=ga[:, 0:ntok], in1=pa[:, 0:ntok], op=ALU.mult),
                 reads=[("R0",), ka], writes=[("R0",)])
            T.op("dve", lambda: nc.vector.tensor_tensor(out=gb[:, 0:ntok], in0=gb[:, 0:ntok], in1=pb[:, 0:ntok], op=ALU.mult),
                 reads=[("R1",), kb], writes=[("R1",)])
            T.op("dve", lambda c=c: nc.vector.tensor_tensor(out=self.mT[:, c, 0:ntok], in0=ga[:, 0:ntok], in1=gb[:, 0:ntok],
                                                            op=ALU.add), reads=[("R0",), ("R1",)], writes=[("mT", c)])
        wo = self.w["w_out"][l]
        for cg in range(4):
            wv, wk = self.load_w(wo[:, cg * 512:(cg + 1) * 512], 16, 512)
            for t in range(ntl):
                rows = rows_list[t]
                ps, pk = self.psum()
                for k in range(16):
                    T.op("pe", lambda k=k, t=t, rows=rows, ps=ps, wv=wv: nc.tensor.matmul(
                        ps[0:rows, :], self.mT[:, k, t * 128:t * 128 + rows], wv[:, k, :], start=(k == 0), stop=(k == 15)),
                        reads=[("mT", k), wk], writes=[pk])
                xs_ = self.x[0:rows, t, cg * 512:(cg + 1) * 512]
                T.op("dve", lambda xs_=xs_, ps=ps, rows=rows: nc.vector.scalar_tensor_tensor(
                    out=xs_, in0=xs_, scalar=ALPHA, in1=ps[0:rows, :], op0=ALU.mult, op1=ALU.add),
                    reads=[("x", t), pk], writes=[("x", t)])
        self.load_lnw("ln1_g", "ln1_b", l)
        for t in range(ntl):
            self.layer_norm(t, rows_list[t], "ln1_g", "ln1_b", l)

    def stage_D(self, l, rows_list):
        T, nc = self.T, self.nc
        ntl = len(rows_list)
        ntok = sum(rows_list)
        wgu, wdn = self.w["w_ffn_gu"][l], self.w["w_ffn_down"][l]
        hT = self.a3[:, 0:8 * TOK].rearrange("p (b s t) -> p b s t", b=2, s=4)
        sg = self.tmp[:, 0:512]
        FC = 512
        nchunk = FF // FC
        for j in range(nchunk):
            buf, bk = self.wbuf()
            wg = buf[:, 0:16 * 512].rearrange("p (k c) -> p k c", k=16)
            bufu, bku = self.wbuf()
            wu = bufu[:, 0:16 * 512].rearrange("p (k c) -> p k c", k=16)
            T.dma("pool", wg, wgu[:, j * FC:(j + 1) * FC].rearrange("(k p) c -> p k c", p=128), writes=[bk], nobar=True)
            T.dma("pool", wu, wgu[:, FF + j * FC:FF + (j + 1) * FC].rearrange("(k p) c -> p k c", p=128), writes=[bku],
                  nobar=True)
            buf2, bk2 = self.wbuf()
            wd = buf2[:, 0:4 * 2048].rearrange("p (s c) -> p s c", s=4)
            T.dma("pool", wd, wdn[j * FC:(j + 1) * FC, :].rearrange("(s p) c -> p s c", p=128), writes=[bk2], nobar=True)
            hb = j % 2
            for s in range(4):
                pg, kg = self.psum()
                pu, ku = self.psum()
                for k in range(16):
                    T.op("pe", lambda k=k, s=s, pg=pg: nc.tensor.matmul(pg[:, 0:ntok], wg[:, k, s * 128:(s + 1) * 128],
                                                                        self.xT[:, k, 0:ntok], start=(k == 0), stop=(k == 15)),
                         reads=[bk, ("xT", "all")], writes=[kg])
                for k in range(16):
                    T.op("pe", lambda k=k, s=s, pu=pu: nc.tensor.matmul(pu[:, 0:ntok], wu[:, k, s * 128:(s + 1) * 128],
                                                                        self.xT[:, k, 0:ntok], start=(k == 0), stop=(k == 15)),
                         reads=[bku, ("xT", "all")], writes=[ku])
                T.op("act", lambda pg=pg: nc.scalar.activation(out=sg[:, 0:ntok], in_=pg[:, 0:ntok], func=AF.Silu),
                     reads=[kg], writes=[("R0",)])
                T.op("dve", lambda pu=pu, s=s, hb=hb: nc.vector.tensor_tensor(out=hT[:, hb, s, 0:ntok], in0=sg[:, 0:ntok],
                                                                              in1=pu[:, 0:ntok], op=ALU.mult),
                     reads=[("R0",), ku], writes=[("hT", hb)])
            for t in range(ntl):
                rows = rows_list[t]
                for cg in range(4):
                    ps, pk = self.psum()
                    for s in range(4):
                        T.op("pe", lambda s=s, t=t, rows=rows, ps=ps, cg=cg, hb=hb: nc.tensor.matmul(
                            ps[0:rows, :], hT[:, hb, s, t * 128:t * 128 + rows], wd[:, s, cg * 512:(cg + 1) * 512],
                            start=(s == 0), stop=(s == 3)), reads=[("hT", hb), bk2], writes=[pk])
                    xs_ = self.x[0:rows, t, cg * 512:(cg + 1) * 512]
                    if j == 0:
                        T.op("dve", lambda xs_=xs_, ps=ps, rows=rows: nc.vector.scalar_tensor_tensor(
                            out=xs_, in0=xs_, scalar=ALPHA, in1=ps[0:rows, :], op0=ALU.mult, op1=ALU.add),
                            reads=[("x", t), pk], writes=[("x", t)])
                    else:
                        T.op("dve", lambda xs_=xs_, ps=ps, rows=rows: nc.vector.tensor_tensor(
                            out=xs_, in0=xs_, in1=ps[0:rows, :], op=ALU.add), reads=[("x", t), pk], writes=[("x", t)])
        self.load_lnw("ln2_g", "ln2_b", l)
        for t in range(ntl):
            self.layer_norm(t, rows_list[t], "ln2_g", "ln2_b", l)

    def stage_E(self, l, rows_list, p_src, last, y_dst):
        T, nc = self.T, self.nc
        ntl = len(rows_list)
        pT = self.a3[:, 0:2 * TOK].rearrange("p (c t) -> p c t", c=2)
        pb = self.a3[:, 2 * TOK:2 * TOK + NT * 256].rearrange("p (t c) -> p t c", t=NT)
        for t in range(ntl):
            rows = rows_list[t]
            T.dma("pool", pb[0:rows, t, :], p_src[t * 128:t * 128 + rows, :], writes=[("pb", t)])
            self.transposes4([pb[0:rows, t, i * 128:(i + 1) * 128] for i in range(2)], rows,
                             pT[:, :, t * 128:t * 128 + rows], [("pb", t)], [("pT", t)])
        bbuf, bbk = self.wbuf(hold=True)
        bias = bbuf[:, 0:4096].bitcast(F32)
        T.dma("sp", bias, self.w["b_ple_gate"][l].partition_broadcast(128), writes=[bbk])
        wg, wp = self.w["w_ple_gate"][l], self.w["w_ple_proj"][l]
        wpv = bbuf[:, 4096:8192].rearrange("p (k c) -> p k c", k=2)
        T.dma("pool", wpv, wp.rearrange("(k p) c -> p k c", p=128), writes=[bbk], nobar=True)
        g1 = self.tmp[:, 0:512]
        for cg in range(4):
            buf, bk = self.wbuf()
            wv = buf[:, 0:16 * 512].rearrange("p (k c) -> p k c", k=16)
            T.dma("pool", wv[:, 0:16, :], wg[:, cg * 512:(cg + 1) * 512].rearrange("(k p) c -> p k c", p=128), writes=[bk],
                  nobar=True)
            for t in range(ntl):
                rows = rows_list[t]
                pg, kg = self.psum()
                pp_, kp = self.psum()
                for k in range(16):
                    T.op("pe", lambda k=k, t=t, rows=rows, pg=pg: nc.tensor.matmul(
                        pg[0:rows, :], self.xT[:, k, t * 128:t * 128 + rows], wv[:, k, :], start=(k == 0), stop=(k == 15)),
                        reads=[("xT", t), bk], writes=[kg])
                for k in range(2):
                    T.op("pe", lambda k=k, t=t, rows=rows, pp_=pp_: nc.tensor.matmul(
                        pp_[0:rows, :], pT[:, k, t * 128:t * 128 + rows], wpv[:, k, cg * 512:(cg + 1) * 512],
                        start=(k == 0), stop=(k == 1)), reads=[("pT", t), bbk], writes=[kp])
                T.op("dve", lambda rows=rows, pg=pg, cg=cg: nc.vector.tensor_tensor(
                    out=g1[0:rows, :], in0=pg[0:rows, :], in1=bias[0:rows, cg * 512:(cg + 1) * 512], op=ALU.add),
                    reads=[kg, bbk], writes=[("R0",)])
                T.op("act", lambda rows=rows: nc.scalar.activation(out=g1[0:rows, :], in_=g1[0:rows, :], func=AF.Sigmoid),
                     reads=[("R0",)], writes=[("R0",)])
                T.op("dve", lambda rows=rows, pp_=pp_: nc.vector.tensor_tensor(out=g1[0:rows, :], in0=g1[0:rows, :],
                                                                               in1=pp_[0:rows, :], op=ALU.mult),
                     reads=[("R0",), kp], writes=[("R0",)])
                xs_ = self.x[0:rows, t, cg * 512:(cg + 1) * 512]
                T.op("dve", lambda xs_=xs_, rows=rows: nc.vector.scalar_tensor_tensor(
                    out=xs_, in0=xs_, scalar=ALPHA, in1=g1[0:rows, :], op0=ALU.mult, op1=ALU.add),
                    reads=[("x", t), ("R0",)], writes=[("x", t)])
        self.release(bbk)
        self.load_lnw("ln3_g", "ln3_b", l)
        for t in range(ntl):
            rows = rows_list[t]
            self.layer_norm(t, rows, "ln3_g", "ln3_b", l, make_xT=not last,
                            out_dram=(y_dst[t * 128:t * 128 + rows, :] if last else None))

    def run_pass(self, g, sample):
        T, nc = self.T, self.nc
        if sample:
            rows_list = [64]
            x_src, y_dst = self.xs, self.y_s
            tab0 = SEQ
        else:
            rows_list = [128] * NT
            x_src, y_dst = self.xp[g * TOK:(g + 1) * TOK, :], self.y_p[g * TOK:(g + 1) * TOK, :]
            tab0 = g * TOK
        ntl = len(rows_list)
        ntok = sum(rows_list)
        T.barrier()
        for t in range(ntl):
            rows = rows_list[t]
            T.dma("sp", self.ropeT[0:rows, t, 0, :], self.cosT[tab0 + t * 128:tab0 + t * 128 + rows, :], writes=[("ropeT",)])
            T.dma("sp", self.ropeT[0:rows, t, 1, :], self.sinT[tab0 + t * 128:tab0 + t * 128 + rows, :], writes=[("ropeT",)])
        T.dma("sp", self.rope2[:, 0, 0:ntok], self.cos2[:, tab0:tab0 + ntok], writes=[("rope2",)])
        T.dma("sp", self.rope2[:, 1, 0:ntok], self.sin2[:, tab0:tab0 + ntok], writes=[("rope2",)])
        self.load_lnw("ln_in_g", "ln_in_b", None)
        for t in range(ntl):
            rows = rows_list[t]
            T.dma("sp", self.x[0:rows, t, :], x_src[t * 128:t * 128 + rows, :], writes=[("x", t)])
            self.layer_norm(t, rows, "ln_in_g", "ln_in_b", None)
        import os
        STOP = int(os.environ.get("KSTOP", "99"))
        if STOP <= 0:
            return
        for l in range(DEPTH):
            if STOP < 90 and l > 0:
                return
            T.barrier()
            if sample:
                self.cache_prep(l)
                T.barrier()
            self.stage_A(l, rows_list, None, g * TOK, sample)
            if STOP <= 1:
                return
            T.barrier()
            if sample:
                segs = []
                for s in (1, 2):
                    blocks_m = [(b, 128, None) for b in range(16)] + [(16, 32, None)]
                    segs.append((s, (s - 1) * 32, 32, blocks_m))
                self.seg_masks = {"mla": None, "sb": 1}
            else:
                segs = []
                for i in range(NT):
                    p = g * NT + i
                    segs.append((0, i * 128, 128, [(b, 128, None) for b in range(p)] + [(p, 128, "diag")]))
            self.stage_B_wrapper(l, segs, sample)
            if STOP <= 2:
                return
            T.barrier()
            self.stage_C(l, rows_list)
            if STOP <= 3:
                return
            T.barrier()
            self.stage_D(l, rows_list)
            if STOP <= 4:
                return
            T.barrier()
            p_src = (self.psm[l] if sample else self.pp[l, g * TOK:(g + 1) * TOK, :])
            self.stage_E(l, rows_list, p_src, l == DEPTH - 1, y_dst)

    def stage_B_wrapper(self, l, segs, sample):
        self._segs = segs
        self._sample = sample
        orig_mla, orig_sb = self.mla_segment, self.sb_segment

        def mla(h, qb, c0, nq, blocks, kT, krT, va, bk, krk):
            bl = [(b, nk, (0 if m == "diag" else None)) for (b, nk, m) in blocks]
            return orig_mla(h, qb, c0, nq, bl, kT, krT, va, bk, krk)

        def sbs(h, c0, nq, blocks, kT, vv, bk):
            if sample:
                bl = [(b, nk, (1 if nk == 32 else None)) for (b, nk, m) in blocks]
            else:
                bl = [(b, nk, (1 if m == "diag" else None)) for (b, nk, m) in blocks]
            return orig_sb(h, c0, nq, bl, kT, vv, bk)

        self.mla_segment, self.sb_segment = mla, sbs
        try:
            self.stage_B(l, segs, 0, sample)
        finally:
            self.mla_segment, self.sb_segment = orig_mla, orig_sb

    def build(self, passes):
        from contextlib import ExitStack
        with ExitStack() as es:
            self.es = es
            self.T.setup(es)
            self.declare()
            self.bounce = self.scratch("bounce", [128, 2624])
            self.alloc()
            self.consts()
            for (g, sample) in passes:
                self.run_pass(g, sample)
            self.T.barrier()
            self.T.finish()


PASSES = [(g, False) for g in range(NPASS)] + [(0, True)]


def build_nc(passes=PASSES):
    nc0 = bass.Bass("TRN2", target_bir_lowering=False)
    t0 = Tracker(nc0, None)
    Builder(nc0, t0).build(passes)
    plan = t0.make_plan()
    nc = bass.Bass("TRN2", target_bir_lowering=False)
    t1 = Tracker(nc, plan)
    Builder(nc, t1).build(passes)
    assert t1.n == plan["n"], (t1.n, plan["n"])
    return nc


def _tables():
    half = ROPE // 2
    inv_freq = (1.0 / (np.float32(10000.0) ** (np.arange(half, dtype=np.float32) * np.float32(2.0 / ROPE)))).astype(np.float32)
    pos = np.concatenate([np.arange(SEQ), PAST + np.arange(32), PAST + np.arange(32)]).astype(np.float32)
    ang = (pos[:, None] * inv_freq[None, :]).astype(np.float32)
    cos, sin = np.cos(ang).astype(np.float32), np.sin(ang).astype(np.float32)
    cos2 = np.ascontiguousarray(np.concatenate([cos, cos], 1).T)
    sin2 = np.ascontiguousarray(np.concatenate([sin, sin], 1).T)
    k = np.arange(128)[:, None]
    q = np.arange(128)[None, :]
    m = np.zeros((128, 3, 128), np.float32)
    m[:, 0, :] = (k // 64) <= (q // 64)
    m[:, 1, :] = k < q
    m[:, 2, :] = k > q
    ident = np.eye(128, dtype=np.float32)
    return cos, sin, cos2, sin2, m.astype(ml_dtypes.bfloat16), ident.astype(ml_dtypes.bfloat16)


_NC_CACHE = {}


def kernel(**inp):
    f = lambda a: np.ascontiguousarray(np.asarray(a, dtype=np.float32))
    if "nc" not in _NC_CACHE:
        _NC_CACHE["nc"] = build_nc()
    nc = _NC_CACHE["nc"]
    cos, sin, cos2, sin2, cmask, ident = _tables()
    wnames = ["ln_in_g", "ln_in_b", "w_in", "b_gate", "q_a_norm_g", "w_q_b", "kv_a_norm_g", "w_kv_b", "w_branch_a",
              "w_branch_b", "w_out", "ln1_g", "ln1_b", "w_ffn_gu", "w_ffn_down", "ln2_g", "ln2_b", "w_ple_gate",
              "b_ple_gate", "w_ple_proj", "ln3_g", "ln3_b"]
    shared = {n: f(inp[n]) for n in wnames}
    shared.update(cosT=cos, sinT=sin, cos2=cos2, sin2=sin2, cmask=cmask, ident=ident)
    xp, xs = f(inp["x_prompt"]), f(inp["x_sample"])
    cl, ck, csk, csv = f(inp["cache_mla_latent"]), f(inp["cache_mla_krope"]), f(inp["cache_sb_k"]), f(inp["cache_sb_v"])
    pp, psm = f(inp["p_prompt"]), f(inp["p_sample"])
    in_maps = []
    for c in range(8):
        b = c // 4
        sb = slice(2 * c, 2 * c + 2)
        m = dict(shared)
        m["xp"] = xp[b]
        m["xs"] = np.ascontiguousarray(xs[sb].reshape(64, D))
        m["c_lat"] = np.ascontiguousarray(cl[:, sb])
        m["c_kr"] = np.ascontiguousarray(ck[:, sb])
        m["c_sbk"] = np.ascontiguousarray(csk[:, sb].reshape(DEPTH, 2, PAST, SBW))
        m["c_sbv"] = np.ascontiguousarray(csv[:, sb].reshape(DEPTH, 2, PAST, SBW))
        m["pp"] = np.ascontiguousarray(pp[:, b])
        m["psm"] = np.ascontiguousarray(psm[:, sb].reshape(DEPTH, 64, PLE))
        in_maps.append(m)
    import os
    ncores = int(os.environ.get("KCORES", "8"))
    if os.environ.get("KTRACE"):
        res = run_bass_kernel_spmd(nc, in_maps[:ncores], core_ids=list(range(ncores)), trace=True)
        print("EXEC_TIME_NS", res.exec_time_ns)
    else:
        res = run_bass_kernel_spmd(nc, in_maps[:ncores], core_ids=list(range(ncores)))
    R = list(res.results)
    while len(R) < 8:
        R.append(R[0])
    y_p = np.stack([R[0]["y_p"], R[4]["y_p"]]).astype(np.float32)
    y_s = np.concatenate([R[c]["y_s"].reshape(2, 32, D) for c in range(8)], 0).astype(np.float32)
    pc = lambda n, shp: np.stack([R[0][n], R[4][n]], 1).reshape(shp).astype(np.float32)
    lat_p = pc("o_lat", (DEPTH, 2, SEQ, KVL))
    kr_p = pc("o_kr", (DEPTH, 2, SEQ, ROPE))
    k_p = pc("o_sbk", (DEPTH, 2, SEQ, SBH, SBD))
    v_p = pc("o_sbv", (DEPTH, 2, SEQ, SBH, SBD))
    sc = lambda n, shp: np.concatenate([R[c][n].reshape(DEPTH, 2, 32, -1) for c in range(8)], 1).reshape(shp).astype(np.float32)
    lat_s = sc("os_lat", (DEPTH, 16, 32, KVL))
    kr_s = sc("os_kr", (DEPTH, 16, 32, ROPE))
    k_s = sc("os_sbk", (DEPTH, 16, 32, SBH, SBD))
    v_s = sc("os_sbv", (DEPTH, 16, 32, SBH, SBD))
    return (y_p, y_s, lat_p, kr_p, k_p, v_p, lat_s, kr_s, k_s, v_s)
```
